# Optimizing a Trainium2 kernel written in Bass

```python
import jax, jax.numpy as jnp
from jax import lax
import numpy as np

D_MODEL = 1024
BATCH = 2
SEQ = 8192
DEPTH = 2

MIX_WIDTH = D_MODEL
POOL_WIDTH = MIX_WIDTH // 2
POOL_WINDOWS = (2, 4, 8, 16)
N_POOL_GROUPS = len(POOL_WINDOWS)
POOL_GROUP_WIDTH = POOL_WIDTH // N_POOL_GROUPS
ATTN_WIDTH = MIX_WIDTH - POOL_WIDTH
HEAD_DIM = 64
N_HEADS = ATTN_WIDTH // HEAD_DIM
IN_WIDTH = POOL_WIDTH + 3 * ATTN_WIDTH + N_HEADS
D_FF = 4 * D_MODEL
BLOCK_Q = 128
LN_EPS = 1e-5
NEG_INF = -1e30
ALPHA = float((2.0 * DEPTH) ** 0.25)
BETA = float((8.0 * DEPTH) ** -0.25)

kernel_name = "hybrid_pool_fox_postnorm_trunk"


def layer_norm(x, g, b):
    xf = x.astype(jnp.float32)
    mu = jnp.mean(xf, axis=-1, keepdims=True)
    var = jnp.mean(jnp.square(xf - mu), axis=-1, keepdims=True)
    y = (xf - mu) * lax.rsqrt(var + LN_EPS)
    return (y * g.astype(jnp.float32) + b.astype(jnp.float32)).astype(x.dtype)


def pool_mixer(u, w, scale):
    B, S, _ = u.shape
    uf = u.astype(jnp.float32)
    c = jnp.cumsum(uf, axis=1)
    pos = jnp.arange(1, S + 1, dtype=jnp.float32)
    diffs = []
    for g, win in enumerate(POOL_WINDOWS):
        sl = slice(g * POOL_GROUP_WIDTH, (g + 1) * POOL_GROUP_WIDTH)
        cg = c[..., sl]
        shifted = jnp.pad(cg, ((0, 0), (win, 0), (0, 0)))[:, :S]
        count = jnp.minimum(pos, float(win))[None, :, None]
        diffs.append((cg - shifted) / count - uf[..., sl])
    d = jnp.concatenate(diffs, axis=-1).astype(u.dtype)
    d = d.reshape(B, S, N_POOL_GROUPS, POOL_GROUP_WIDTH)
    y = jnp.einsum('bsgc,gcd->bsgd', d, w).reshape(B, S, POOL_WIDTH)
    return y * scale


def forgetting_attention(q, k, v, f_logit):
    B, S, _ = q.shape
    to_heads = lambda t: t.reshape(B, S, N_HEADS, HEAD_DIM).transpose(0, 2, 1, 3)
    q, k, v = to_heads(q), to_heads(k), to_heads(v)
    log_f = jax.nn.log_sigmoid(f_logit.astype(jnp.float32)).transpose(0, 2, 1)
    cf = jnp.cumsum(log_f, axis=-1)
    scale = HEAD_DIM ** -0.5
    kpos = jnp.arange(S)
    n_blocks = S // BLOCK_Q

    def one_block(i):
        start = i * BLOCK_Q
        qb = lax.dynamic_slice_in_dim(q, start, BLOCK_Q, axis=2)
        cq = lax.dynamic_slice_in_dim(cf, start, BLOCK_Q, axis=2)
        qpos = start + jnp.arange(BLOCK_Q)
        s = jnp.einsum('bhqd,bhkd->bhqk', qb, k, preferred_element_type=jnp.float32) * scale
        s = s + cq[..., :, None] - cf[..., None, :]
        s = jnp.where(kpos[None, :] <= qpos[:, None], s, NEG_INF)
        p = jax.nn.softmax(s, axis=-1)
        return jnp.einsum('bhqk,bhkd->bhqd', p.astype(v.dtype), v)

    out = lax.map(one_block, jnp.arange(n_blocks))
    out = out.transpose(1, 0, 3, 2, 4).reshape(B, S, ATTN_WIDTH)
    return out


def hybrid_mixer(x, w_in, b_f, pool_w, pool_scale, w_out):
    h = x @ w_in
    o = 0
    u = h[..., o:o + POOL_WIDTH]; o += POOL_WIDTH
    q = h[..., o:o + ATTN_WIDTH]; o += ATTN_WIDTH
    k = h[..., o:o + ATTN_WIDTH]; o += ATTN_WIDTH
    v = h[..., o:o + ATTN_WIDTH]; o += ATTN_WIDTH
    f_logit = h[..., o:o + N_HEADS] + b_f
    pool_out = pool_mixer(u, pool_w, pool_scale)
    attn_out = forgetting_attention(q, k, v, f_logit)
    return jnp.concatenate([pool_out, attn_out], axis=-1) @ w_out


def setup_inputs(seed: int = 0) -> dict:
    key = jax.random.key(seed)
    ks = jax.random.split(key, 12)
    f32 = jnp.float32
    x = jax.random.normal(ks[0], (BATCH, SEQ, D_MODEL), f32)
    w_in = jax.random.normal(ks[1], (DEPTH, D_MODEL, IN_WIDTH), f32) * D_MODEL ** -0.5
    col_scale = np.ones((IN_WIDTH,), np.float32)
    col_scale[:POOL_WIDTH] = BETA
    col_scale[POOL_WIDTH + 2 * ATTN_WIDTH:POOL_WIDTH + 3 * ATTN_WIDTH] = BETA
    w_in = w_in * jnp.asarray(col_scale)
    b_f = 3.0 + 0.5 * jax.random.normal(ks[2], (DEPTH, N_HEADS), f32)
    pool_w = jax.random.normal(ks[3], (DEPTH, N_POOL_GROUPS, POOL_GROUP_WIDTH, POOL_GROUP_WIDTH), f32) * POOL_GROUP_WIDTH ** -0.5
    pool_scale = 1.0 + 0.1 * jax.random.normal(ks[4], (DEPTH, POOL_WIDTH), f32)
    w_out = jax.random.normal(ks[5], (DEPTH, MIX_WIDTH, D_MODEL), f32) * (MIX_WIDTH ** -0.5 * BETA)
    ln1_g = 1.0 + 0.05 * jax.random.normal(ks[6], (DEPTH, D_MODEL), f32)
    ln1_b = 0.02 * jax.random.normal(ks[7], (DEPTH, D_MODEL), f32)
    w_mlp1 = jax.random.normal(ks[8], (DEPTH, D_MODEL, D_FF), f32) * D_MODEL ** -0.5
    w_mlp2 = jax.random.normal(ks[9], (DEPTH, D_FF, D_MODEL), f32) * (D_FF ** -0.5 * BETA)
    ln2_g = 1.0 + 0.05 * jax.random.normal(ks[10], (DEPTH, D_MODEL), f32)
    ln2_b = 0.02 * jax.random.normal(ks[11], (DEPTH, D_MODEL), f32)
    return {"x": x, "w_in": w_in, "b_f": b_f, "pool_w": pool_w, "pool_scale": pool_scale,
            "w_out": w_out, "ln1_g": ln1_g, "ln1_b": ln1_b, "w_mlp1": w_mlp1,
            "w_mlp2": w_mlp2, "ln2_g": ln2_g, "ln2_b": ln2_b}


def reference(x, w_in, b_f, pool_w, pool_scale, w_out, ln1_g, ln1_b, w_mlp1, w_mlp2, ln2_g, ln2_b):
    for l in range(DEPTH):
        mix = hybrid_mixer(x, w_in[l], b_f[l], pool_w[l], pool_scale[l], w_out[l])
        x = layer_norm(ALPHA * x + mix, ln1_g[l], ln1_b[l])
        hid = jnp.square(jax.nn.relu(x @ w_mlp1[l]))
        x = layer_norm(ALPHA * x + hid @ w_mlp2[l], ln2_g[l], ln2_b[l])
    return x
```

```python
import contextlib
import numpy as np
import ml_dtypes
import concourse.bass as bass
import concourse.mybir as mybir
from concourse.bass_utils import run_bass_kernel_spmd

F32 = mybir.dt.float32
BF16 = mybir.dt.bfloat16
ACT = mybir.ActivationFunctionType
ALU = mybir.AluOpType
AX = mybir.AxisListType

D = 1024
S = 8192
B = 2
DEPTH = 2
NH = 8
HD = 64
DFF = 4096
INW = 2056
T = 2048
NG = 16
NB = 64
LN_EPS = 1e-5
ALPHA = float((2.0 * DEPTH) ** 0.25)
WINS = (2, 4, 8, 16)
NCORES = 8


class Sched:
    ENGS = ("pe", "act", "dve", "pool", "sp")

    def __init__(self, nc, stack):
        self.nc = nc
        self.stack = stack
        self.ops = {e: [] for e in self.ENGS}
        self.cnt = {e: 0 for e in self.ENGS}
        self.known = {e: {} for e in self.ENGS}
        self.lastw = {}
        self.readers = {}
        self.sem = {e: stack.enter_context(nc.semaphore("prog_" + e)) for e in ("pe", "act", "dve", "pool")}
        self.dsem = {}
        self.dcnt = {}

    def _dsem(self, name):
        if name not in self.dsem:
            self.dsem[name] = self.stack.enter_context(self.nc.semaphore("d_" + name))
            self.dcnt[name] = 0
        return self.dsem[name]

    def _deps(self, e, reads, writes):
        toks = []
        for k in reads:
            if k in self.lastw:
                toks.append(self.lastw[k])
        for k in writes:
            if k in self.lastw:
                toks.append(self.lastw[k])
            toks.extend(self.readers.get(k, ()))
        waits = []
        kn = self.known[e]
        for t in toks:
            if t[0] == "eng":
                _, e2, n2 = t
                if e2 == e and e == "pe":
                    continue
                key = ("eng", e2)
            else:
                _, e2, n2 = t
                key = ("dma", e2)
            if kn.get(key, 0) < n2:
                kn[key] = n2
                waits.append((key, n2))
        best = {}
        for key, n2 in waits:
            best[key] = max(best.get(key, 0), n2)
        out = []
        for key, n2 in best.items():
            s = self.sem[key[1]] if key[0] == "eng" else self.dsem[key[1]]
            out.append((s, n2))
        return out

    def op(self, e, fn, reads=(), writes=()):
        waits = self._deps(e, reads, writes)
        self.cnt[e] += 1
        n = self.cnt[e]
        self.ops[e].append((waits, fn, self.sem[e]))
        tok = ("eng", e, n)
        for k in reads:
            self.readers.setdefault(k, []).append(tok)
        for k in writes:
            self.lastw[k] = tok
            self.readers[k] = []

    def dma(self, q, semname, out, in_, reads=(), writes=(), slow=False):
        s = self._dsem(semname)
        waits = self._deps(q, reads, writes)
        self.dcnt[semname] += 16
        c = self.dcnt[semname]

        def fn(eng, out=out, in_=in_, s=s, slow=slow):
            if slow:
                eng.dma_start(out=out, in_=in_, allow_slow_non_contiguous=True).then_inc(s, 16)
            else:
                eng.dma_start(out=out, in_=in_).then_inc(s, 16)
            return None
        self.ops[q].append((waits, fn, None))
        tok = ("dma", semname, c)
        for k in reads:
            self.readers.setdefault(k, []).append(tok)
        for k in writes:
            self.lastw[k] = tok
            self.readers[k] = []

    def collective(self, semname, fn, reads=(), writes=()):
        s = self._dsem(semname)
        waits = self._deps("pool", reads, writes)
        self.dcnt[semname] += 1
        c = self.dcnt[semname]

        def f2(eng, fn=fn, s=s):
            fn(eng).then_inc(s, 1)
            return None
        self.ops["pool"].append((waits, f2, None))
        tok = ("dma", semname, c)
        for k in reads:
            self.readers.setdefault(k, []).append(tok)
        for k in writes:
            self.lastw[k] = tok
            self.readers[k] = []

    def barrier(self, exclude=("cc",)):
        for e in self.ENGS:
            kn = self.known[e]
            waits = []
            for e2 in ("pe", "act", "dve", "pool"):
                n2 = self.cnt[e2]
                if n2 > 0 and kn.get(("eng", e2), 0) < n2:
                    kn[("eng", e2)] = n2
                    waits.append((self.sem[e2], n2))
            for nm, c in self.dcnt.items():
                if nm in exclude:
                    continue
                if c > 0 and kn.get(("dma", nm), 0) < c:
                    kn[("dma", nm)] = c
                    waits.append((self.dsem[nm], c))
            if waits:
                self.ops[e].append((waits, None, None))

    def final_wait(self, q, semnames):
        for nm in semnames:
            s, c = self.dsem[nm], self.dcnt[nm]
            self.ops[q].append(([(s, c)], None, None))

    def emit(self, e, eng):
        for waits, fn, inc in self.ops[e]:
            for s, v in waits:
                eng.wait_ge(s, v)
            if fn is None:
                continue
            r = fn(eng)
            if inc is not None:
                r.then_inc(inc, 1)


def build_program(n_layers, fused, dbg=False):
    nc = bass.Bass("TRN2", target_bir_lowering=False)
    L = n_layers
    dbgt = {}
    if dbg:
        for nm, shp, dt in (("d_negc", [128, 512], F32), ("d_QT", [65, NH * T], BF16), ("d_mixT", [128, 8 * T], BF16),
                            ("d_x1", [128, 16 * D], F32), ("d_XT", [D, S], BF16), ("d_KT", [65, 2 * S], BF16),
                            ("d_V", [128, NB * 192], BF16), ("d_lfo", [8, T], F32), ("d_uT", [128, 4 * 528], F32),
                            ("d_diffT", [128, 4 * 512], BF16)):
            dbgt[nm] = nc.dram_tensor(nm, shp, dt, kind="ExternalOutput")

    def din(name, shape, dt=F32):
        return nc.dram_tensor(name, list(shape), dt, kind="ExternalInput")

    xfull = din("xfull", [S, D])
    xown = din("xown", [T, D])
    w_in = din("w_in", [L, D, INW])
    b_f = din("b_f", [L, NH])
    pool_w = din("pool_w", [L, 4, 128, 128])
    pool_scale = din("pool_scale", [L, 512])
    w_out = din("w_out", [L, D, D])
    ln1_g = din("ln1_g", [L, D])
    ln1_b = din("ln1_b", [L, D])
    w1 = din("w_mlp1", [L, D, DFF])
    w2 = din("w_mlp2", [L, DFF, D])
    ln2_g = din("ln2_g", [L, D])
    ln2_b = din("ln2_b", [L, D])
    c_mask = din("c_mask", [128, 16 * 512], BF16)
    c_halosel = din("c_halosel", [128, 64])
    c_prefsel = din("c_prefsel", [8, 64])
    c_invc = din("c_invc", [128, 256])
    c_identf = din("c_identf", [128, 128])
    c_swap = din("c_swap", [128, 128])
    y = nc.dram_tensor("y", [T, D], F32, kind="ExternalOutput")
    XT = nc.dram_tensor("XT", [NG, D, 512], BF16)
    W1b = nc.dram_tensor("W1b", [8, 128, 8 * 512], BF16)
    W2b = nc.dram_tensor("W2b", [8, 128, 8 * 512], BF16)
    xown1 = nc.dram_tensor("xown1", [T, D], F32)
    agin = [nc.dram_tensor(f"agin{m}", [D, 512], BF16) for m in range(4)]
    agout = [nc.dram_tensor(f"agout{m}", [4 * D, 512], BF16) for m in range(4)]

    with contextlib.ExitStack() as st:
        sc = Sched(nc, st)

        def sb(name, shape, dt):
            return st.enter_context(nc.sbuf_tensor(name, list(shape), dt))

        ps = st.enter_context(nc.psum_tensor("ps", [128, 8, 512], F32))

        mixT = sb("mixT", [128, 8, T], BF16)
        identf = sb("identf", [128, 128], F32)
        swapm = sb("swapm", [128, 128], F32)
        halosel = sb("halosel", [128, 64], F32)
        prefsel = sb("prefsel", [8, 64], F32)
        invc = sb("invc", [128, 256], F32)
        negc = sb("negc", [128, NB, NH], F32)
        pscale = sb("pscale", [128, 4], F32)
        bft = sb("bft", [8, 1], F32)
        negbf = sb("negbf", [8, 1], F32)
        small = sb("small", [128, 64], F32)

        sc.dma("sp", "c_identf", identf[:], c_identf[:, :], writes=["identf"])
        sc.dma("sp", "c_swapm", swapm[:], c_swap[:, :], writes=["swapm"])
        sc.dma("sp", "c_halosel", halosel[:], c_halosel[:, :], writes=["halosel"])
        sc.dma("sp", "c_prefsel", prefsel[:], c_prefsel[:, :], writes=["prefsel"])
        sc.dma("sp", "c_invc", invc[:], c_invc[:, :], writes=["invc"])

        for l in range(L):
            layer(nc, sc, st, l, locals())

        sc.final_wait("sp", ["yout"] + ["dbg_" + k for k in dbgt])

        with nc.Block() as block:
            @block.tensor
            def _(e):
                sc.emit("pe", e)

            @block.scalar
            def _(e):
                sc.emit("act", e)

            @block.vector
            def _(e):
                sc.emit("dve", e)

            @block.gpsimd
            def _(e):
                sc.emit("pool", e)

            @block.sync
            def _(e):
                sc.emit("sp", e)
    return nc


def layer(nc, sc, st_outer, l, env):
    g = env
    ps = g["ps"]
    mixT, identf, swapm = g["mixT"], g["identf"], g["swapm"]
    halosel, prefsel, invc, negc = g["halosel"], g["prefsel"], g["invc"], g["negc"]
    pscale, bft, negbf, small = g["pscale"], g["bft"], g["negbf"], g["small"]
    xfull, xown, XT, y = g["xfull"], g["xown"], g["XT"], g["y"]
    w_in, b_f, pool_w, pool_scale, w_out = g["w_in"], g["b_f"], g["pool_w"], g["pool_scale"], g["w_out"]
    ln1_g, ln1_b, w1, w2, ln2_g, ln2_b = g["ln1_g"], g["ln1_b"], g["w1"], g["w2"], g["ln2_g"], g["ln2_b"]
    c_mask = g["c_mask"]
    dbgt = g["dbgt"]
    L = f"L{l}_"
    nlayers = g["L"]
    last = (l == nlayers - 1)
    if l > 0:
        xown = g["xown1"]
    agin, agout, xown1 = g["agin"], g["agout"], g["xown1"]

    def dump(name, src, reads):
        if name in dbgt:
            sc.dma("sp", "dbg_" + name, dbgt[name].ap(), src, reads=reads, writes=["dbgout_" + name])

    def PS(b, n=1):
        if n == 1:
            return ps[:, b, :]
        return ps[:, b:b + n, :]

    def psk(b):
        return f"ps{b}"

    win_v = w_in[l].rearrange("(kc p) c -> p kc c", p=128)

    sc.dma("sp", "c_pscale", pscale[:], pool_scale[l].rearrange("(g p) -> p g", p=128), writes=["pscale"], slow=True)
    sc.dma("sp", "c_bft", bft[:], b_f[l].rearrange("(h o) -> h o", o=1), writes=["bft"])
    sc.op("dve", lambda e: e.tensor_scalar(out=negbf[:], in0=bft[:], scalar1=-1.0, scalar2=None, op0=ALU.mult),
          reads=["bft"], writes=["negbf"])

    XTv = XT.ap().rearrange("c (p kc) t -> c p kc t", kc=8)
    agov = [a.ap().rearrange("(r p kc) t -> r p kc t", r=4, kc=8) for a in agout]
    aginv = [a.ap().rearrange("(p kc) t -> p kc t", kc=8) for a in agin]
    W1b, W2b = g["W1b"], g["W2b"]

    def xt_chunk(c):
        if l == 0:
            return XTv[c]
        return agov[c // 4][c % 4]

    with contextlib.ExitStack() as s1:
        def sb(name, shape, dt):
            return s1.enter_context(nc.sbuf_tensor(L + name, list(shape), dt))

        QT = sb("QT", [65, NH, T], BF16)
        xtail = sb("xtail", [128, 8, NG * 16], BF16)
        gs = sb("gs", [8, NG], F32)
        xTc = [sb(f"xTc{i}", [128, 8, 512], BF16) for i in range(2)]
        Wf = sb("Wf", [128, 8, 8], BF16)
        sc.dma("pool", "wf", Wf[:], win_v[:, :, 2048:2056], writes=["Wf"])
        sF = contextlib.ExitStack()
        lfT = sF.enter_context(nc.sbuf_tensor(L + "lfT", [8, S], F32))
        cumT = sF.enter_context(nc.sbuf_tensor(L + "cumT", [8, S], F32))
        zer = sF.enter_context(nc.sbuf_tensor(L + "zer", [8, 512], F32))
        sc.op("pool", lambda e: e.memset(zer[:], 0.0), writes=["zer"])

        def f_tail(c):
            sl = slice(c * 512, (c + 1) * 512)
            sc.op("act", lambda e, o=lfT[:, sl]: e.activation(out=o, in_=o, func=ACT.Ln, bias=1.0, scale=1.0),
                  reads=[f"lfT{c}"], writes=[f"lfT{c}"])
            sc.op("dve", lambda e, o=gs[:, c:c + 1], i=lfT[:, sl]: e.tensor_reduce(out=o, in_=i, axis=AX.X, op=ALU.add),
                  reads=[f"lfT{c}"], writes=["gs"])
            init = 0.0 if c == 0 else cumT[:, c * 512 - 1:c * 512]
            sc.op("dve", lambda e, o=cumT[:, sl], d0=lfT[:, sl], ini=init:
                  e.tensor_tensor_scan(out=o, data0=d0, data1=zer[:], initial=ini, op0=ALU.add, op1=ALU.add),
                  reads=[f"lfT{c}", "zer", "cumT"], writes=["cumT"])

        if l == 0:
            with contextlib.ExitStack() as sA:
                xin = [sA.enter_context(nc.sbuf_tensor(L + f"xin{i}", [128, 4, D], F32)) for i in range(2)]
                xts = [sA.enter_context(nc.sbuf_tensor(L + f"xts{i}", [128, 8, 512], BF16)) for i in range(2)]
                def loadA(c):
                    sc.dma("sp", f"xin{c % 2}", xin[c % 2][:], xfull[c * 512:(c + 1) * 512, :].rearrange("(b p) d -> p b d", p=128),
                           writes=[f"xin{c % 2}"])
                loadA(0)
                for c in range(NG):
                    s_ = c % 2
                    if c + 1 < NG:
                        loadA(c + 1)
                    for blk in range(4):
                        pb = (blk % 2) * 2
                        for kc in range(8):
                            sc.op("pe", lambda e, o=ps[:, pb + kc // 4, (kc % 4) * 128:(kc % 4 + 1) * 128],
                                  i=xin[s_][:, blk, kc * 128:(kc + 1) * 128]: e.transpose(out=o, in_=i, identity=identf[:]),
                                  reads=[f"xin{s_}", "identf"], writes=[psk(pb + kc // 4)])
                        src = ps[:, pb:pb + 2, :].rearrange("p a (k t) -> p (a k) t", t=128)
                        dst = xts[s_][:, :, blk * 128:(blk + 1) * 128]
                        if blk % 2 == 0:
                            sc.op("act", lambda e, o=dst, i=src: e.activation(out=o, in_=i, func=ACT.Copy),
                                  reads=[psk(pb), psk(pb + 1)], writes=[f"xts{s_}"])
                        else:
                            sc.op("dve", lambda e, o=dst, i=src: e.tensor_copy(out=o, in_=i),
                                  reads=[psk(pb), psk(pb + 1)], writes=[f"xts{s_}"])
                    sc.op("pool", lambda e, o=xtail[:, :, c * 16:(c + 1) * 16], i=xts[s_][:, :, 496:512]: e.tensor_copy(out=o, in_=i),
                          reads=[f"xts{s_}"], writes=["xtail"])
                    sc.dma("sp", f"xts{s_}", XTv[c], xts[s_][:], reads=[f"xts{s_}"], writes=["XT"])
                    pbf = 4 + c % 2
                    for kc in range(8):
                        sc.op("pe", lambda e, o=ps[0:8, pbf, :], a=Wf[:, kc, :], b=xts[s_][:, kc, :], k=kc:
                              e.matmul(o, lhsT=a, rhs=b, start=(k == 0), stop=(k == 7)),
                              reads=["Wf", f"xts{s_}"], writes=[psk(pbf)])
                    sc.op("act", lambda e, o=lfT[:, c * 512:(c + 1) * 512], i=ps[0:8, pbf, :]:
                          e.activation(out=o, in_=i, func=ACT.Exp, bias=negbf[:], scale=-1.0),
                          reads=[psk(pbf), "negbf"], writes=[f"lfT{c}"])
                    f_tail(c)


            sc.barrier()
        with contextlib.ExitStack() as sA:
            xin = [sA.enter_context(nc.sbuf_tensor(L + f"pxin{i}", [128, 4, D], F32)) for i in range(1)] * 2
            xts = [sA.enter_context(nc.sbuf_tensor(L + f"pxts{i}", [128, 8, 512], BF16)) for i in range(2)]
            Wq = sA.enter_context(nc.sbuf_tensor(L + "Wq", [128, 8, 512], BF16))
            sc.dma("pool", "wq", Wq[:], win_v[:, :, 512:1024], writes=["Wq"])

            def loadX(m):
                sc.dma("sp", "pxin", xin[0][:], xown[m * 512:(m + 1) * 512, :].rearrange("(b p) d -> p b d", p=128),
                       reads=["xown1"], writes=["pxin"])
            if l == 0:
                loadX(0)
            for m in range(4):
                s_ = m % 2
                if l > 0:
                    if m == 0:
                        sc.dma("sp", f"xts{s_}", xts[s_][:], aginv[m], reads=[f"agin{m}"], writes=[f"xts{s_}"])
                    if m + 1 < 4:
                        sc.dma("sp", f"xts{(m + 1) % 2}", xts[(m + 1) % 2][:], aginv[m + 1], reads=[f"agin{m + 1}"], writes=[f"xts{(m + 1) % 2}"])
                for blk in (range(4) if l == 0 else ()):
                    pb = (blk % 2) * 2
                    for kc in range(8):
                        sc.op("pe", lambda e, o=ps[:, pb + kc // 4, (kc % 4) * 128:(kc % 4 + 1) * 128],
                              i=xin[0][:, blk, kc * 128:(kc + 1) * 128]: e.transpose(out=o, in_=i, identity=identf[:]),
                              reads=["pxin", "identf"], writes=[psk(pb + kc // 4)])
                    src = ps[:, pb:pb + 2, :].rearrange("p a (k t) -> p (a k) t", t=128)
                    dst = xts[s_][:, :, blk * 128:(blk + 1) * 128]
                    if blk % 2 == 0:
                        sc.op("act", lambda e, o=dst, i=src: e.activation(out=o, in_=i, func=ACT.Copy),
                              reads=[psk(pb), psk(pb + 1)], writes=[f"xts{s_}"])
                    else:
                        sc.op("dve", lambda e, o=dst, i=src: e.tensor_copy(out=o, in_=i),
                              reads=[psk(pb), psk(pb + 1)], writes=[f"xts{s_}"])
                if l == 0 and m + 1 < 4:
                    loadX(m + 1)
                if l == 0:
                    sc.dma("sp", f"xts{s_}", aginv[m], xts[s_][:], reads=[f"xts{s_}"], writes=[f"agin{m}"])
                if False:
                    sc.collective("cc", lambda e, m=m: e.collective_compute("AllGather", ALU.bypass, replica_groups=[[0, 1, 2, 3], [4, 5, 6, 7]],
                                                                            ins=[agin[m].ap().opt()], outs=[agout[m].ap().opt()]),
                                  reads=[f"agin{m}"], writes=["XT"])
                for h in range(NH):
                    pb = 4 + h % 2
                    for kc in range(8):
                        sc.op("pe", lambda e, o=ps[0:64, pb, :], a=Wq[:, kc, h * 64:(h + 1) * 64], b=xts[s_][:, kc, :], k=kc:
                              e.matmul(o, lhsT=a, rhs=b, start=(k == 0), stop=(k == 7)),
                              reads=["Wq", f"xts{s_}"], writes=[psk(pb)])
                    sc.op("act", lambda e, o=QT[0:64, h, m * 512:(m + 1) * 512], i=ps[0:64, pb, :]:
                          e.activation(out=o, in_=i, func=ACT.Copy, scale=0.125),
                          reads=[psk(pb)], writes=[f"QT{h}"])
        sc.barrier()
        with contextlib.ExitStack() as sC:
            def loadC(c):
                sc.dma("sp", f"xTc{c % 2}", xTc[c % 2][:], xt_chunk(c), reads=["XT"], writes=[f"xTc{c % 2}"])
            if l > 0:
                loadC(0)
            for c in (range(NG) if l > 0 else ()):
                s_ = c % 2
                if c + 1 < NG:
                    loadC(c + 1)
                pb = c % 2
                if l > 0:
                    sc.op("pool", lambda e, o=xtail[:, :, c * 16:(c + 1) * 16], i=xTc[s_][:, :, 496:512]: e.tensor_copy(out=o, in_=i),
                          reads=[f"xTc{s_}"], writes=["xtail"])
                for kc in range(8):
                    sc.op("pe", lambda e, o=ps[0:8, pb, :], a=Wf[:, kc, :], b=xTc[s_][:, kc, :], k=kc:
                          e.matmul(o, lhsT=a, rhs=b, start=(k == 0), stop=(k == 7)),
                          reads=["Wf", f"xTc{s_}"], writes=[psk(pb)])
                sc.op("act", lambda e, o=lfT[:, c * 512:(c + 1) * 512], i=ps[0:8, pb, :]:
                      e.activation(out=o, in_=i, func=ACT.Exp, bias=negbf[:], scale=-1.0),
                      reads=[psk(pb), "negbf"], writes=[f"lfT{c}"])
                f_tail(c)
            for blk in range(NB):
                sc.op("pe", lambda e, o=ps[:, 2, blk * 8:(blk + 1) * 8], i=cumT[:, blk * 128:(blk + 1) * 128]:
                      e.transpose(out=o, in_=i, identity=identf[0:8, 0:8]),
                      reads=["cumT", "identf"], writes=[psk(2)])
            sc.op("dve", lambda e: e.tensor_copy(out=negc[:].rearrange("p b h -> p (b h)"), in_=ps[:, 2, :]),
                  reads=[psk(2)], writes=["negc"])

        sc.barrier()
        sF.close()
        mask = sb("mask", [128, 16, 512], BF16)
        Wk = [sb(f"Wk{i}", [128, 8, 128], BF16) for i in range(2)]
        Wv = [sb(f"Wv{i}", [128, 8, 128], BF16) for i in range(2)]
        sc.dma("sp", "c_mask", mask[:], c_mask[:, :].rearrange("p (j q) -> p j q", q=512), writes=["mask"])
        for w_ in range(1):
            sc.dma("pool", f"wk{w_}", Wk[w_][:], win_v[:, :, 1024:1024 + 128], writes=[f"Wk{w_}"])
            sc.dma("pool", f"wv{w_}", Wv[w_][:], win_v[:, :, 1536:1536 + 128], writes=[f"Wv{w_}"])
        sc.dma("sp", "xTc0", xTc[0][:], xt_chunk(0), reads=["XT"], writes=["xTc0"])
        Wu = sb("Wu", [128, 8, 512], BF16)
        pw = sb("pw", [128, 4, 128], BF16)
        sc.dma("pool", "wu", Wu[:], win_v[:, :, 0:512], writes=["Wu"])
        sc.dma("pool", "pw", pw[:], pool_w[l].rearrange("g c d -> c g d"), writes=["pw"])
        dump("d_negc", negc[:].rearrange("p b h -> p (b h)"), ["negc"])
        with contextlib.ExitStack() as sB:
            def sbB(name, shape, dt):
                return sB.enter_context(nc.sbuf_tensor(L + name, list(shape), dt))
            xTo = [sbB(f"xTo{i}", [128, 8, 512], BF16) for i in range(2)]
            uT = sbB("uT", [128, 4, 528], F32)
            s2 = sbB("s2", [128, 4, 528], F32)
            s4 = sbB("s4", [128, 4, 528], F32)
            utail = sbB("utail", [128, 4, NG * 16], F32)
            htmp = sbB("htmp", [128, 4, 16, 16], F32)
            diffT = sbB("diffT", [128, 4, 512], BF16)
            dtmp = sbB("dtmp", [128, 4, 16], F32)
            lfo = sbB("lfo", [8, T], F32)
            cumo = sbB("cumo", [8, T], F32)
            aq = sbB("aq", [8, T], BF16)
            pref = sbB("pref", [8, 4], F32)
            ptmp = sbB("ptmp", [8, NG], F32)
            zerB = sbB("zerB", [8, 512], F32)
            sc.op("pool", lambda e: e.memset(zerB[:], 0.0), writes=["zerB"])

            for gq in range(4):
                for kc in range(8):
                    sc.op("pe", lambda e, o=ps[:, 3, 0:256], a=Wu[:, kc, gq * 128:(gq + 1) * 128], b=xtail[:, kc, :], k=kc:
                          e.matmul(o, lhsT=a, rhs=b, start=(k == 0), stop=(k == 7)),
                          reads=["Wu", "xtail"], writes=[psk(3)])
                sc.op("dve", lambda e, o=utail[:, gq, :], i=ps[:, 3, 0:256]: e.tensor_copy(out=o, in_=i),
                      reads=[psk(3)], writes=["utail"])

            for m in range(4):
                s_ = m % 2
                if m == 0:
                    sc.dma("sp", f"xTo{s_}", xTo[s_][:], aginv[m], reads=[f"agin{m}"], writes=[f"xTo{s_}"])
                if m + 1 < 4:
                    sc.dma("sp", f"xTo{(m + 1) % 2}", xTo[(m + 1) % 2][:], aginv[m + 1], reads=[f"agin{m + 1}"], writes=[f"xTo{(m + 1) % 2}"])
                for kc in range(8):
                    sc.op("pe", lambda e, o=ps[0:8, 6, :], a=Wf[:, kc, :], b=xTo[s_][:, kc, :], k=kc:
                          e.matmul(o, lhsT=a, rhs=b, start=(k == 0), stop=(k == 7)),
                          reads=["Wf", f"xTo{s_}"], writes=[psk(6)])
                sc.op("act", lambda e, o=lfo[:, m * 512:(m + 1) * 512], i=ps[0:8, 6, :]:
                      e.activation(out=o, in_=i, func=ACT.Exp, bias=negbf[:], scale=-1.0),
                      reads=[psk(6), "negbf"], writes=["lfo"])
                sc.op("act", lambda e, o=lfo[:, m * 512:(m + 1) * 512]:
                      e.activation(out=o, in_=o, func=ACT.Ln, bias=1.0, scale=1.0),
                      reads=["lfo"], writes=["lfo"])
                sc.op("dve", lambda e, i1=prefsel[:, m * 16:(m + 1) * 16]: e.tensor_tensor(out=ptmp[:], in0=gs[:], in1=i1, op=ALU.mult),
                      reads=["gs", "prefsel"], writes=["ptmp"])
                sc.op("dve", lambda e, o=pref[:, m:m + 1]: e.tensor_reduce(out=o, in_=ptmp[:], axis=AX.X, op=ALU.add),
                      reads=["ptmp"], writes=["pref"])
                sc.op("dve", lambda e, o=cumo[:, m * 512:(m + 1) * 512], d0=lfo[:, m * 512:(m + 1) * 512], ini=pref[:, m:m + 1]:
                      e.tensor_tensor_scan(out=o, data0=d0, data1=zerB[:], initial=ini, op0=ALU.add, op1=ALU.add),
                      reads=["lfo", "zerB", "pref"], writes=["cumo"])
                sc.op("dve", lambda e, o=aq[:, m * 512:(m + 1) * 512], i=cumo[:, m * 512:(m + 1) * 512]:
                      e.tensor_scalar(out=o, in0=i, scalar1=-1.0, scalar2=None, op0=ALU.mult),
                      reads=["cumo"], writes=["aq"])

                for gq in range(4):
                    pb = 4 + gq % 2
                    for kc in range(8):
                        sc.op("pe", lambda e, o=ps[:, pb, :], a=Wu[:, kc, gq * 128:(gq + 1) * 128], b=xTo[s_][:, kc, :], k=kc:
                              e.matmul(o, lhsT=a, rhs=b, start=(k == 0), stop=(k == 7)),
                              reads=["Wu", f"xTo{s_}"], writes=[psk(pb)])
                    sc.op("act", lambda e, o=uT[:, gq, 16:528], i=ps[:, pb, :]: e.activation(out=o, in_=i, func=ACT.Copy),
                          reads=[psk(pb)], writes=["uT"])
                sc.op("dve", lambda e, sel=halosel[:, m * 16:(m + 1) * 16].unsqueeze(1).unsqueeze(1).to_broadcast([128, 4, 16, 16]):
                      e.tensor_tensor(out=htmp[:], in0=utail[:].rearrange("p g (G t) -> p g t G", t=16), in1=sel, op=ALU.mult),
                      reads=["utail", "halosel"], writes=["htmp"])
                sc.op("dve", lambda e: e.tensor_reduce(out=uT[:, :, 0:16], in_=htmp[:], axis=AX.X, op=ALU.add),
                      reads=["htmp", "uT"], writes=["uT"])
                sc.op("dve", lambda e: e.tensor_tensor(out=s2[:, :, 1:528], in0=uT[:, :, 1:528], in1=uT[:, :, 0:527], op=ALU.add),
                      reads=["uT"], writes=["s2"])
                sc.op("dve", lambda e: e.tensor_tensor(out=s4[:, 1:4, 3:528], in0=s2[:, 1:4, 3:528], in1=s2[:, 1:4, 1:526], op=ALU.add),
                      reads=["s2"], writes=["s4"])
                def diff(gq, ssum, w, m=m):
                    sc.op("dve", lambda e: e.scalar_tensor_tensor(out=diffT[:, gq, 16:512], in0=ssum[:, gq, 32:528], scalar=1.0 / w,
                                                                  in1=uT[:, gq, 32:528], op0=ALU.mult, op1=ALU.subtract),
                          reads=["s2", "s4", "uT"], writes=["diffT"])
                    sc.op("dve", lambda e: e.tensor_tensor(out=dtmp[:, gq, :], in0=ssum[:, gq, 16:32],
                                                           in1=invc[:, (m * 4 + gq) * 16:(m * 4 + gq + 1) * 16], op=ALU.mult),
                          reads=["s2", "s4", "invc"], writes=["dtmp"])
                    sc.op("dve", lambda e: e.tensor_tensor(out=diffT[:, gq, 0:16], in0=dtmp[:, gq, :], in1=uT[:, gq, 16:32], op=ALU.subtract),
                          reads=["dtmp", "uT"], writes=["diffT"])
                diff(0, s2, 2)
                diff(1, s4, 4)
                sc.op("dve", lambda e: e.tensor_tensor(out=s2[:, 2:4, 7:528], in0=s4[:, 2:4, 7:528], in1=s4[:, 2:4, 3:524], op=ALU.add),
                      reads=["s4", "diffT", "dtmp"], writes=["s2"])
                diff(2, s2, 8)
                sc.op("dve", lambda e: e.tensor_tensor(out=s4[:, 3:4, 15:528], in0=s2[:, 3:4, 15:528], in1=s2[:, 3:4, 7:520], op=ALU.add),
                      reads=["s2", "diffT", "dtmp"], writes=["s4"])
                diff(3, s4, 16)
                for gq in range(4):
                    pb = 4 + gq % 2
                    sc.op("pe", lambda e, o=ps[:, pb, :], a=pw[:, gq, :], b=diffT[:, gq, :]: e.matmul(o, lhsT=a, rhs=b, start=True, stop=True),
                          reads=["pw", "diffT"], writes=[psk(pb)])
                    sc.op("act", lambda e, o=mixT[:, gq, m * 512:(m + 1) * 512], i=ps[:, pb, :], s=pscale[:, gq:gq + 1]:
                          e.activation(out=o, in_=i, func=ACT.Copy, scale=s),
                          reads=[psk(pb), "pscale"], writes=[f"mixT{gq}"])
            for h in range(NH):
                sc.dma("sp", "aqmv", QT[64:65, h, :], aq[h:h + 1, :], reads=["aq"], writes=["QTaug"])
            dump("d_QT", QT[:].rearrange("p h t -> p (h t)"), [f"QT{h}" for h in range(NH)] + ["QTaug"])
            dump("d_lfo", cumo[:], ["cumo"])
            dump("d_uT", uT[:].rearrange("p g t -> p (g t)"), ["uT"])
            dump("d_diffT", diffT[:].rearrange("p g t -> p (g t)"), ["diffT"])

        sc.barrier()
        with contextlib.ExitStack() as sT:
            def sbT(name, shape, dt):
                return sT.enter_context(nc.sbuf_tensor(L + name, list(shape), dt))
            KT = sbT("KT", [65, 2, S], BF16)
            Vaug = sbT("Vaug", [128, NB, 192], BF16)
            Pt = [sbT(f"Pt{i}", [128, 1024], BF16) for i in range(2)]
            RA = sbT("RA", [128, 512], F32)
            RB = sbT("RB", [128, 512], F32)
            Wsb = sbT("Wsb", [128, 512], F32)
            ktmp = [sbT(f"ktmp{i}", [128, 512], BF16) for i in range(2)]
            sc.op("pool", lambda e: e.memset(KT[64:65, :, :], 1.0), writes=["KTaug"])
            sc.op("pool", lambda e: e.memset(Vaug[:, :, 64:128], 1.0), writes=["Vones"])

            def loadW(hp_):
                w_ = hp_ % 2
                sc.dma("pool", f"wk{w_}", Wk[w_][:], win_v[:, :, 1024 + hp_ * 128:1024 + (hp_ + 1) * 128], writes=[f"Wk{w_}"])
                sc.dma("pool", f"wv{w_}", Wv[w_][:], win_v[:, :, 1536 + hp_ * 128:1536 + (hp_ + 1) * 128], writes=[f"Wv{w_}"])

            xTcA = [xTc[0], xTc[1], sbT("xTc2", [128, 8, 512], BF16), sbT("xTc3", [128, 8, 512], BF16)]

            def loadK(c):
                sc.dma("sp", f"xTc{c % 4}", xTcA[c % 4][:], xt_chunk(c), reads=["XT"], writes=[f"xTc{c % 4}"])
            for c_ in (1, 2, 3):
                loadK(c_)
            w1v_ = w1[l].rearrange("(kc p) c -> p kc c", p=128)
            w2v_ = w2[l].rearrange("(f p) c -> p f c", p=128)
            casts = []
            for s_i in range(8):
                casts.append(("wcast1", W1b.ap()[s_i].rearrange("p (k c) -> p k c", c=512), w1v_[:, :, s_i * 512:(s_i + 1) * 512], "W1b"))
            for half in range(2):
                for sc_ in range(4):
                    casts.append(("wcast2", W2b.ap()[half * 4 + sc_].rearrange("p (k c) -> p k c", c=512),
                                  w2v_[:, sc_ * 8:(sc_ + 1) * 8, half * 512:(half + 1) * 512], "W2b"))
            pending = [None]
            for hp in range(4):
                ws = hp % 2
                for c in range(NG):
                    s_ = c % 4
                    pb = c % 2
                    kt_ = c % 2
                    for kc in range(8):
                        sc.op("pe", lambda e, o=ps[:, pb, :], a=Wk[ws][:, kc, :], b=xTcA[s_][:, kc, :], k=kc:
                              e.matmul(o, lhsT=a, rhs=b, start=(k == 0), stop=(k == 7)),
                              reads=[f"Wk{ws}", f"xTc{s_}"], writes=[psk(pb)])
                    sc.op("act", lambda e, o=KT[0:64, 0, c * 512:(c + 1) * 512], i=ps[0:64, pb, :]:
                          e.activation(out=o, in_=i, func=ACT.Copy),
                          reads=[psk(pb)], writes=["KT0"])
                    sc.op("act", lambda e, o=ktmp[kt_][64:128, :], i=ps[64:128, pb, :]: e.activation(out=o, in_=i, func=ACT.Copy),
                          reads=[psk(pb)], writes=[f"ktmp{kt_}"])
                    sc.dma("sp", f"ktB{kt_}", KT[0:64, 1, c * 512:(c + 1) * 512], ktmp[kt_][64:128, :],
                           reads=[f"ktmp{kt_}"], writes=[f"KT1_{kt_}"])
                    pb = 2
                    for blk in range(4):
                        for kc in range(8):
                            sc.op("pe", lambda e, o=ps[:, pb, blk * 128:(blk + 1) * 128], a=xTcA[s_][:, kc, blk * 128:(blk + 1) * 128],
                                  b=Wv[ws][:, kc, :], k=kc: e.matmul(o, lhsT=a, rhs=b, start=(k == 0), stop=(k == 7)),
                                  reads=[f"Wv{ws}", f"xTc{s_}"], writes=[psk(pb)])
                    pv = ps[:, pb, :].rearrange("p (b c) -> p b c", c=128)
                    sc.op("dve", lambda e, o=Vaug[:, c * 4:(c + 1) * 4, 0:64], i=pv[:, :, 0:64]: e.tensor_copy(out=o, in_=i),
                          reads=[psk(pb)], writes=["VA"])
                    sc.op("dve", lambda e, o=Vaug[:, c * 4:(c + 1) * 4, 128:192], i=pv[:, :, 64:128]: e.tensor_copy(out=o, in_=i),
                          reads=[psk(pb)], writes=["VB"])
                    if c + 4 < NG:
                        loadK(c + 4)

                if hp + 1 < 4:
                    loadW(hp + 1)
                    for c_ in range(4):
                        loadK(c_)
                for sem_, dst_, src_, key_ in casts[hp * 4:(hp + 1) * 4]:
                    sc.dma("pool", sem_, dst_, src_, reads=["VB", "KT0"], writes=[key_])
                if hp == 0:
                    dump("d_KT", KT[:].rearrange("p h t -> p (h t)"), ["KT0", "KT1_0", "KT1_1", "KTaug"])
                    dump("d_V", Vaug[:].rearrange("p b c -> p (b c)"), ["VA", "VB", "Vones"])
                for hh in range(2):
                    h = 2 * hp + hh
                    vk = "VA" if hh == 0 else "VB"
                    ktk = (["KT0"] if hh == 0 else ["KT1_0", "KT1_1"]) + ["KTaug", f"QT{h}", "QTaug"]
                    if hh == 0:
                        R, sums, outs, rk = RA, slice(64, 128), slice(0, 64), "RA"
                    else:
                        R, sums, outs, rk = RB, slice(0, 64), slice(64, 128), "RB"

                    def normalise(m_, ob_):
                        sc.op("dve", lambda e, o=R[sums, :], i=ps[sums, ob_, :]: e.reciprocal(out=o, in_=i),
                              reads=[psk(ob_), rk], writes=[rk])
                        sc.dma("sp", "nrm", Wsb[outs, :], R[sums, :], reads=[rk], writes=["Wsb"])

                        def fin(o=mixT[outs, 4 + hp, m_ * 512:(m_ + 1) * 512], i0=ps[outs, ob_, :], i1=Wsb[outs, :], ob_=ob_, key=f"mixT{4 + hp}_{hh}"):
                            sc.op("dve", lambda e: e.tensor_tensor(out=o, in0=i0, in1=i1, op=ALU.mult),
                                  reads=[psk(ob_), "Wsb"], writes=[key])
                        assert pending[0] is None
                        pending[0] = fin

                    def flush():
                        if pending[0] is not None:
                            pending[0]()
                            pending[0] = None

                    for pr in range(2):
                        m0, m1 = 2 * pr, 2 * pr + 1
                        NJ0, NJ1 = 16 * m0 + 16, 16 * m1 + 16
                        obA, obB = 4 + 2 * pr, 5 + 2 * pr
                        qA = QT[0:65, h, m0 * 512:(m0 + 1) * 512]
                        qB = QT[0:65, h, m1 * 512:(m1 + 1) * 512]

                        def qk(j, qA=qA, qB=qB, NJ0=NJ0):
                            w = j % 2
                            kt_ap = KT[0:65, hh, j * 128:(j + 1) * 128]
                            if j < NJ0:
                                sc.op("pe", lambda e, o=ps[:, 2 * w, :], a=kt_ap, q=qA: e.matmul(o, lhsT=a, rhs=q, start=True, stop=True),
                                      reads=ktk, writes=[psk(2 * w)])
                            sc.op("pe", lambda e, o=ps[:, 2 * w + 1, :], a=kt_ap, q=qB: e.matmul(o, lhsT=a, rhs=q, start=True, stop=True),
                                  reads=ktk, writes=[psk(2 * w + 1)])
                        qk(0)
                        for j in range(NJ1):
                            if j == 6 or j == NJ0 + 6:
                                flush()
                            if j + 1 < NJ1:
                                qk(j + 1)
                            w = j % 2
                            bia = negc[:, j, h:h + 1]
                            if j < NJ0:
                                sc.op("act", lambda e, o=Pt[w][:, :], i=ps[:, 2 * w:2 * w + 2, :].rearrange("p a c -> p (a c)"), bia=bia:
                                      e.activation(out=o, in_=i, func=ACT.Exp, bias=bia, scale=1.0),
                                      reads=[psk(2 * w), psk(2 * w + 1), "negc"], writes=[f"Pt{w}"])
                                if j >= 16 * m0:
                                    sc.op("dve", lambda e, o=Pt[w][:, 0:512], mk=mask[:, j - 16 * m0, :]:
                                          e.tensor_tensor(out=o, in0=o, in1=mk, op=ALU.min),
                                          reads=[f"Pt{w}", "mask"], writes=[f"Pt{w}"])
                                sc.op("pe", lambda e, o=ps[:, obA, :], a=Vaug[:, j, hh * 64:hh * 64 + 128], b=Pt[w][:, 0:512], jj=j, NJ0=NJ0:
                                      e.matmul(o, lhsT=a, rhs=b, start=(jj == 0), stop=(jj == NJ0 - 1)),
                                      reads=[vk, "Vones", f"Pt{w}"], writes=[psk(obA)])
                            else:
                                sc.op("act", lambda e, o=Pt[w][:, 512:1024], i=ps[:, 2 * w + 1, :], bia=bia:
                                      e.activation(out=o, in_=i, func=ACT.Exp, bias=bia, scale=1.0),
                                      reads=[psk(2 * w + 1), "negc"], writes=[f"Pt{w}"])
                                sc.op("dve", lambda e, o=Pt[w][:, 512:1024], mk=mask[:, j - 16 * m1, :]:
                                      e.tensor_tensor(out=o, in0=o, in1=mk, op=ALU.min),
                                      reads=[f"Pt{w}", "mask"], writes=[f"Pt{w}"])
                            sc.op("pe", lambda e, o=ps[:, obB, :], a=Vaug[:, j, hh * 64:hh * 64 + 128], b=Pt[w][:, 512:1024], jj=j, NJ1=NJ1:
                                  e.matmul(o, lhsT=a, rhs=b, start=(jj == 0), stop=(jj == NJ1 - 1)),
                                  reads=[vk, "Vones", f"Pt{w}"], writes=[psk(obB)])
                            if j == NJ0 - 1:
                                flush()
                                normalise(m0, obA)
                        flush()
                        normalise(m1, obB)
            if pending[0] is not None:
                pending[0]()
                pending[0] = None

    sc.barrier()
    with contextlib.ExitStack() as s2_:
        def sb2(name, shape, dt):
            return s2_.enter_context(nc.sbuf_tensor(L + name, list(shape), dt))
        xres = sb2("xres", [128, 16, D], F32)
        zt = sb2("zt", [128, 4, D], F32)
        xn = sb2("xn", [128, D], F32)
        stt = sb2("stt", [128, 12], F32)
        gb = [None] * 4
        sD = contextlib.ExitStack()
        Wo = sD.enter_context(nc.sbuf_tensor(L + "Wo", [128, 8, D], BF16))
        gb[0] = sD.enter_context(nc.sbuf_tensor(L + "gb0", [128, D], F32))
        gb[1] = sD.enter_context(nc.sbuf_tensor(L + "gb1", [128, D], F32))

        sc.dma("sp", "xres", xres[:], xown.ap().rearrange("(lb p) d -> p lb d", p=128), reads=["xown1"], writes=["xres"])
        sc.dma("pool", "wo", Wo[:], w_out[l].rearrange("(mc p) c -> p mc c", p=128), writes=["Wo"])
        for i, v in enumerate((ln1_g, ln1_b)):
            sc.dma("sp", f"gb{i}", gb[i][:], v[l].partition_broadcast(128), writes=[f"gb{i}"])

        mixkeys = [f"mixT{i}" for i in range(4)] + [f"mixT{4 + hp}_{hh}" for hp in range(4) for hh in range(2)]

        def ln_steps(zsrc, zkey, gi, bi, lb, final):
            st_ = []
            st_.append(lambda: sc.op("dve", lambda e: e.bn_stats(out=stt[:, 0:6], in_=zsrc[:, 0:512]), reads=[zkey], writes=["stt"]))
            st_.append(lambda: sc.op("dve", lambda e: e.bn_stats(out=stt[:, 6:12], in_=zsrc[:, 512:1024]), reads=[zkey, "stt"], writes=["stt"]))
            st_.append(lambda: sc.op("dve", lambda e: e.bn_aggr(out=small[:, 0:2], in_=stt[:, :]), reads=["stt"], writes=["small"]))
            st_.append(lambda: sc.op("act", lambda e: e.activation(out=small[:, 2:3], in_=small[:, 1:2], func=ACT.Sqrt, bias=small[:, 8:9], scale=1.0),
                                     reads=["small", "epsc"], writes=["small_sd"]))
            st_.append(lambda: sc.op("dve", lambda e: e.reciprocal(out=small[:, 3:4], in_=small[:, 2:3]), reads=["small_sd"], writes=["small_r"]))
            st_.append(lambda: sc.op("dve", lambda e: e.tensor_scalar(out=small[:, 4:5], in0=small[:, 0:1], scalar1=small[:, 3:4], scalar2=-1.0,
                                                                      op0=ALU.mult, op1=ALU.mult),
                                     reads=["small", "small_r"], writes=["small_nm"]))
            st_.append(lambda: sc.op("act", lambda e: e.activation(out=xn[:], in_=zsrc, func=ACT.Identity, scale=small[:, 3:4], bias=small[:, 4:5]),
                                     reads=[zkey, "small_r", "small_nm"], writes=["xn"]))
            st_.append(lambda: sc.op("pool", lambda e: e.tensor_tensor(out=xn[:], in0=xn[:], in1=gb[gi][:], op=ALU.mult),
                                     reads=["xn", f"gb{gi}"], writes=["xn"]))

            def last_():
                sc.op("pool", lambda e: e.tensor_tensor(out=xres[:, lb, :], in0=xn[:], in1=gb[bi][:], op=ALU.add),
                      reads=["xn", f"gb{bi}", f"xres{lb}", "xres"], writes=[f"xres{lb}"])
                if final and last:
                    sc.dma("sp", "yout", y[lb * 128:(lb + 1) * 128, :], xres[:, lb, :], reads=[f"xres{lb}"], writes=[f"y{lb}"])
                elif final:
                    sc.dma("sp", "x1out", xown1[lb * 128:(lb + 1) * 128, :], xres[:, lb, :], reads=[f"xres{lb}"], writes=["xown1"])
            st_.append(last_)
            return st_

        def layernorm(zsrc, zkey, gi, bi, lb, final):
            for f_ in ln_steps(zsrc, zkey, gi, bi, lb, final):
                f_()

        sc.op("pool", lambda e: e.memset(small[:, 8:9], LN_EPS), writes=["epsc"])
        dump("d_mixT", mixT[:].rearrange("p c t -> p (c t)"), mixkeys)

        for lb in range(16):
            for half in range(2):
                pb = (lb % 2) * 2 + half
                for mc in range(8):
                    sc.op("pe", lambda e, o=ps[:, pb, :], a=mixT[:, mc, lb * 128:(lb + 1) * 128], b=Wo[:, mc, half * 512:(half + 1) * 512], k=mc:
                          e.matmul(o, lhsT=a, rhs=b, start=(k == 0), stop=(k == 7)),
                          reads=mixkeys + ["Wo"], writes=[psk(pb)])
                sc.op("dve", lambda e, o=zt[:, lb % 2, half * 512:(half + 1) * 512], i0=xres[:, lb, half * 512:(half + 1) * 512], i1=ps[:, pb, :]:
                      e.scalar_tensor_tensor(out=o, in0=i0, scalar=ALPHA, in1=i1, op0=ALU.mult, op1=ALU.add),
                      reads=["xres", f"xres{lb}", psk(pb)], writes=[f"zt{lb % 2}"])
            layernorm(zt[:, lb % 2, :], f"zt{lb % 2}", 0, 1, lb, False)

        dump("d_x1", xres[:].rearrange("p b d -> p (b d)"), [f"xres{lb}" for lb in range(16)])
        sc.barrier()
        sD.close()
        gb[2] = sb2("gb2", [128, D], F32)
        gb[3] = sb2("gb3", [128, D], F32)
        x1T = sb2("x1T", [128, 8, 512], BF16)
        hidT = sb2("hidT", [128, 32, 512], BF16)
        rtmp = [sb2(f"rtmp{i}", [128, 512], F32) for i in range(2)]
        NSLOT = 4
        wsl = [sb2(f"wsl{i}", [128, 8, 512], BF16) for i in range(NSLOT)]
        for i, v in ((2, ln2_g), (3, ln2_b)):
            sc.dma("sp", f"gb{i}", gb[i][:], v[l].partition_broadcast(128), writes=[f"gb{i}"])
        w1v = w1[l].rearrange("(kc p) c -> p kc c", p=128)
        w2v = w2[l].rearrange("(f p) c -> p f c", p=128)
        slot_ctr = [0]

        def wload(src):
            i = slot_ctr[0] % NSLOT
            slot_ctr[0] += 1
            sc.dma("sp", f"wsl{i}", wsl[i][:], src[0], reads=[src[1]], writes=[f"wsl{i}"])
            return i

        wsrcs = []
        for m in range(4):
            for s_ in range(8):
                wsrcs.append((W1b.ap()[s_].rearrange("p (k c) -> p k c", c=512), "W1b"))
            for half in range(2):
                for sc_ in range(4):
                    wsrcs.append((W2b.ap()[half * 4 + sc_].rearrange("p (k c) -> p k c", c=512), "W2b"))
        issued = [0]
        used = [0]

        def wnext():
            while issued[0] < len(wsrcs) and issued[0] < used[0] + NSLOT - 1:
                wload(wsrcs[issued[0]])
                issued[0] += 1
            i = used[0] % NSLOT
            used[0] += 1
            return i

        def x1T_build(m):
            for blk in range(4):
                lb = m * 4 + blk
                pb = 6
                for kc in range(8):
                    sc.op("pe", lambda e, o=ps[:, pb + kc // 4, (kc % 4) * 128:(kc % 4 + 1) * 128],
                          i=xres[:, lb, kc * 128:(kc + 1) * 128]: e.transpose(out=o, in_=i, identity=identf[:]),
                          reads=[f"xres{lb}", "identf"], writes=[psk(pb + kc // 4)])
                src = ps[:, pb:pb + 2, :].rearrange("p a (k t) -> p (a k) t", t=128)
                sc.op("act", lambda e, o=x1T[:, :, blk * 128:(blk + 1) * 128], i=src: e.activation(out=o, in_=i, func=ACT.Copy),
                      reads=[psk(pb), psk(pb + 1)], writes=["x1T"])

        def x2T_gather(g_):
            for blk in range(4):
                lb = g_ * 4 + blk
                pb = 6
                for kc in range(8):
                    sc.op("pe", lambda e, o=ps[:, pb + kc // 4, (kc % 4) * 128:(kc % 4 + 1) * 128],
                          i=xres[:, lb, kc * 128:(kc + 1) * 128]: e.transpose(out=o, in_=i, identity=identf[:]),
                          reads=[f"xres{lb}", "identf"], writes=[psk(pb + kc // 4)])
                src = ps[:, pb:pb + 2, :].rearrange("p a (k t) -> p (a k) t", t=128)
                sc.op("act", lambda e, o=x1T[:, :, blk * 128:(blk + 1) * 128], i=src: e.activation(out=o, in_=i, func=ACT.Copy),
                      reads=[psk(pb), psk(pb + 1)], writes=["x1T"])
            sc.dma("sp", "x2T", aginv[g_], x1T[:], reads=["x1T"], writes=[f"agin{g_}"])
            sc.collective("cc", lambda e, g_=g_: e.collective_compute("AllGather", ALU.bypass, replica_groups=[[0, 1, 2, 3], [4, 5, 6, 7]],
                                                                       ins=[agin[g_].ap().opt()], outs=[agout[g_].ap().opt()], dma_qos="P3"),
                          reads=[f"agin{g_}"], writes=["XT"])

        pend = []
        x1T_build(0)
        for m in range(4):
            for s_ in range(8):
                si = wnext()
                for q in range(4):
                    fc = s_ * 4 + q
                    pb = 4 + fc % 2
                    for kc in range(8):
                        sc.op("pe", lambda e, o=ps[:, pb, :], a=wsl[si][:, kc, q * 128:(q + 1) * 128], b=x1T[:, kc, :], k=kc:
                              e.matmul(o, lhsT=a, rhs=b, start=(k == 0), stop=(k == 7)),
                              reads=[f"wsl{si}", "x1T"], writes=[psk(pb)])
                    rt = rtmp[fc % 2]
                    sc.op("act", lambda e, o=rt[:], i=ps[:, pb, :]: e.activation(out=o, in_=i, func=ACT.Relu),
                          reads=[psk(pb)], writes=[f"rtmp{fc % 2}"])
                    sc.op("dve", lambda e, o=hidT[:, fc, :], i=rt[:]: e.tensor_tensor(out=o, in0=i, in1=i, op=ALU.mult),
                          reads=[f"rtmp{fc % 2}"], writes=[f"hid{fc}"])
                    for _ in range(2):
                        if pend:
                            pend.pop(0)()
            while pend:
                pend.pop(0)()
            if not last and m >= 1:
                x2T_gather(m - 1)
            hidkeys = [f"hid{fc}" for fc in range(32)]
            for half in range(2):
                for sc_ in range(4):
                    si = wnext()
                    for blk in range(4):
                        for fq in range(8):
                            fc = sc_ * 8 + fq
                            sc.op("pe", lambda e, o=ps[:, blk, :], a=hidT[:, fc, blk * 128:(blk + 1) * 128], b=wsl[si][:, fq, :], k=fc:
                                  e.matmul(o, lhsT=a, rhs=b, start=(k == 0), stop=(k == 31)),
                                  reads=hidkeys + [f"wsl{si}"], writes=[psk(blk)])
                for blk in range(4):
                    lb = m * 4 + blk
                    sc.op("dve", lambda e, o=zt[:, blk, half * 512:(half + 1) * 512], i0=xres[:, lb, half * 512:(half + 1) * 512], i1=ps[:, blk, :]:
                          e.scalar_tensor_tensor(out=o, in0=i0, scalar=ALPHA, in1=i1, op0=ALU.mult, op1=ALU.add),
                          reads=[f"xres{lb}", psk(blk)], writes=[f"zt{blk}"])
            if m + 1 < 4:
                x1T_build(m + 1)
            for blk in range(4):
                pend += ln_steps(zt[:, blk, :], f"zt{blk}", 2, 3, m * 4 + blk, True)
        while pend:
            pend.pop(0)()
        if not last:
            x2T_gather(3)
        sc.barrier()


def _core_consts(r):
    k = np.arange(128)[:, None, None]
    jj = np.arange(16)[None, :, None]
    q = np.arange(512)[None, None, :]
    mask = (((jj * 128 + k) <= (r * 512 + q)).astype(np.float32) * 3e38).astype(ml_dtypes.bfloat16).reshape(128, 16 * 512)
    halosel = np.zeros((4, 16), np.float32)
    prefsel = np.zeros((4, 16), np.float32)
    invc = np.zeros((4, 4, 16), np.float32)
    for m in range(4):
        G = 4 * m + r
        if G >= 1:
            halosel[m, G - 1] = 1.0
        prefsel[m, :G] = 1.0
        for g, w in enumerate(WINS):
            pos = G * 512 + np.arange(16) + 1
            invc[m, g] = 1.0 / np.minimum(pos, w)
    return {
        "c_mask": mask,
        "c_halosel": np.ascontiguousarray(np.broadcast_to(halosel.reshape(1, 64), (128, 64))),
        "c_prefsel": np.ascontiguousarray(np.broadcast_to(prefsel.reshape(1, 64), (8, 64))),
        "c_invc": np.ascontiguousarray(np.broadcast_to(invc.reshape(1, 256), (128, 256))),
        "c_identf": np.eye(128, dtype=np.float32),
        "c_swap": np.roll(np.eye(128, dtype=np.float32), 64, axis=0),
    }


def _own_rows(r):
    return np.concatenate([np.arange((4 * m + r) * 512, (4 * m + r + 1) * 512) for m in range(4)])


_PROG = {}
FUSED = True


def kernel(x, w_in, b_f, pool_w, pool_scale, w_out, ln1_g, ln1_b, w_mlp1, w_mlp2, ln2_g, ln2_b):
    f32 = lambda a: np.ascontiguousarray(np.asarray(a, dtype=np.float32))
    x = f32(x)
    ws = dict(w_in=f32(w_in), b_f=f32(b_f), pool_w=f32(pool_w), pool_scale=f32(pool_scale), w_out=f32(w_out),
              ln1_g=f32(ln1_g), ln1_b=f32(ln1_b), w_mlp1=f32(w_mlp1), w_mlp2=f32(w_mlp2), ln2_g=f32(ln2_g), ln2_b=f32(ln2_b))
    consts = [_core_consts(c % 4) for c in range(NCORES)]
    rows = [_own_rows(c % 4) for c in range(NCORES)]
    if FUSED:
        if "fused" not in _PROG:
            _PROG["fused"] = build_program(DEPTH, True)
        nc = _PROG["fused"]
        in_maps = []
        for c in range(NCORES):
            b = c // 4
            m = {"xfull": x[b], "xown": np.ascontiguousarray(x[b][rows[c]])}
            m.update(ws)
            m.update(consts[c])
            in_maps.append(m)
        res = run_bass_kernel_spmd(nc, in_maps, core_ids=list(range(NCORES)))
        out = np.empty_like(x)
        for c in range(NCORES):
            out[c // 4][rows[c]] = res.results[c]["y"]
        return out
    if "unfused" not in _PROG:
        _PROG["unfused"] = build_program(1, False)
    nc = _PROG["unfused"]
    cur = x
    for l in range(DEPTH):
        in_maps = []
        for c in range(NCORES):
            b = c // 4
            m = {"xfull": cur[b], "xown": np.ascontiguousarray(cur[b][rows[c]])}
            for k, v in ws.items():
                m[k] = np.ascontiguousarray(v[l:l + 1])
            m.update(consts[c])
            in_maps.append(m)
        res = run_bass_kernel_spmd(nc, in_maps, core_ids=list(range(NCORES)))
        nxt = np.empty_like(cur)
        for c in range(NCORES):
            nxt[c // 4][rows[c]] = res.results[c]["y"]
        cur = nxt
    return cur
```

```python
import contextlib
import numpy as np
import ml_dtypes
import concourse.bass as bass
import concourse.mybir as mybir
from concourse.bass_utils import run_bass_kernel_spmd

F32 = mybir.dt.float32
BF16 = mybir.dt.bfloat16
ACT = mybir.ActivationFunctionType
ALU = mybir.AluOpType
AX = mybir.AxisListType

D = 1024
S = 8192
B = 2
DEPTH = 2
NH = 8
HD = 64
DFF = 4096
INW = 2056
T = 2048
NG = 16
NB = 64
LN_EPS = 1e-5
ALPHA = float((2.0 * DEPTH) ** 0.25)
WINS = (2, 4, 8, 16)
NCORES = 8


class Sched:
    ENGS = ("pe", "act", "dve", "pool", "sp")

    def __init__(self, nc, stack):
        self.nc = nc
        self.stack = stack
        self.ops = {e: [] for e in self.ENGS}
        self.cnt = {e: 0 for e in self.ENGS}
        self.known = {e: {} for e in self.ENGS}
        self.lastw = {}
        self.readers = {}
        self.sem = {e: stack.enter_context(nc.semaphore("prog_" + e)) for e in ("pe", "act", "dve", "pool")}
        self.dsem = {}
        self.dcnt = {}

    def _dsem(self, name):
        if name not in self.dsem:
            self.dsem[name] = self.stack.enter_context(self.nc.semaphore("d_" + name))
            self.dcnt[name] = 0
        return self.dsem[name]

    def _deps(self, e, reads, writes):
        toks = []
        for k in reads:
            if k in self.lastw:
                toks.append(self.lastw[k])
        for k in writes:
            if k in self.lastw:
                toks.append(self.lastw[k])
            toks.extend(self.readers.get(k, ()))
        waits = []
        kn = self.known[e]
        for t in toks:
            if t[0] == "eng":
                _, e2, n2 = t
                if e2 == e and e == "pe":
                    continue
                key = ("eng", e2)
            else:
                _, e2, n2 = t
                key = ("dma", e2)
            if kn.get(key, 0) < n2:
                kn[key] = n2
                waits.append((key, n2))
        best = {}
        for key, n2 in waits:
            best[key] = max(best.get(key, 0), n2)
        out = []
        for key, n2 in best.items():
            s = self.sem[key[1]] if key[0] == "eng" else self.dsem[key[1]]
            out.append((s, n2))
        return out

    def op(self, e, fn, reads=(), writes=()):
        waits = self._deps(e, reads, writes)
        self.cnt[e] += 1
        n = self.cnt[e]
        self.ops[e].append((waits, fn, self.sem[e]))
        tok = ("eng", e, n)
        for k in reads:
            self.readers.setdefault(k, []).append(tok)
        for k in writes:
            self.lastw[k] = tok
            self.readers[k] = []

    def dma(self, q, semname, out, in_, reads=(), writes=(), slow=False):
        s = self._dsem(semname)
        waits = self._deps(q, reads, writes)
        self.dcnt[semname] += 16
        c = self.dcnt[semname]

        def fn(eng, out=out, in_=in_, s=s, slow=slow):
            if slow:
                eng.dma_start(out=out, in_=in_, allow_slow_non_contiguous=True).then_inc(s, 16)
            else:
                eng.dma_start(out=out, in_=in_).then_inc(s, 16)
            return None
        self.ops[q].append((waits, fn, None))
        tok = ("dma", semname, c)
        for k in reads:
            self.readers.setdefault(k, []).append(tok)
        for k in writes:
            self.lastw[k] = tok
            self.readers[k] = []

    def collective(self, semname, fn, reads=(), writes=()):
        s = self._dsem(semname)
        waits = self._deps("pool", reads, writes)
        self.dcnt[semname] += 1
        c = self.dcnt[semname]

        def f2(eng, fn=fn, s=s):
            fn(eng).then_inc(s, 1)
            return None
        self.ops["pool"].append((waits, f2, None))
        tok = ("dma", semname, c)
        for k in reads:
            self.readers.setdefault(k, []).append(tok)
        for k in writes:
            self.lastw[k] = tok
            self.readers[k] = []

    def barrier(self, exclude=("cc",)):
        for e in self.ENGS:
            kn = self.known[e]
            waits = []
            for e2 in ("pe", "act", "dve", "pool"):
                n2 = self.cnt[e2]
                if n2 > 0 and kn.get(("eng", e2), 0) < n2:
                    kn[("eng", e2)] = n2
                    waits.append((self.sem[e2], n2))
            for nm, c in self.dcnt.items():
                if nm in exclude:
                    continue
                if c > 0 and kn.get(("dma", nm), 0) < c:
                    kn[("dma", nm)] = c
                    waits.append((self.dsem[nm], c))
            if waits:
                self.ops[e].append((waits, None, None))

    def final_wait(self, q, semnames):
        for nm in semnames:
            s, c = self.dsem[nm], self.dcnt[nm]
            self.ops[q].append(([(s, c)], None, None))

    def emit(self, e, eng):
        for waits, fn, inc in self.ops[e]:
            for s, v in waits:
                eng.wait_ge(s, v)
            if fn is None:
                continue
            r = fn(eng)
            if inc is not None:
                r.then_inc(inc, 1)


def build_program(n_layers, fused, dbg=False):
    nc = bass.Bass("TRN2", target_bir_lowering=False)
    L = n_layers
    dbgt = {}
    if dbg:
        for nm, shp, dt in (("d_negc", [128, 512], F32), ("d_QT", [65, NH * T], BF16), ("d_mixT", [128, 8 * T], BF16),
                            ("d_x1", [128, 16 * D], F32), ("d_XT", [D, S], BF16), ("d_KT", [65, 2 * S], BF16),
                            ("d_V", [128, NB * 192], BF16), ("d_lfo", [8, T], F32), ("d_uT", [128, 4 * 528], F32),
                            ("d_diffT", [128, 4 * 512], BF16)):
            dbgt[nm] = nc.dram_tensor(nm, shp, dt, kind="ExternalOutput")

    def din(name, shape, dt=F32):
        return nc.dram_tensor(name, list(shape), dt, kind="ExternalInput")

    xTfull = din("xTfull", [D, S])
    xownT = din("xownT", [D, T])
    xown = din("xown", [T, D])
    w_in = din("w_in", [L, D, INW])
    b_f = din("b_f", [L, NH])
    pool_w = din("pool_w", [L, 4, 128, 128])
    pool_scale = din("pool_scale", [L, 512])
    w_out = din("w_out", [L, D, D])
    ln1_g = din("ln1_g", [L, D])
    ln1_b = din("ln1_b", [L, D])
    w1 = din("w_mlp1", [L, D, DFF])
    w2 = din("w_mlp2", [L, DFF, D])
    ln2_g = din("ln2_g", [L, D])
    ln2_b = din("ln2_b", [L, D])
    c_mask = din("c_mask", [128, 16 * 512], BF16)
    c_halosel = din("c_halosel", [128, 64])
    c_prefsel = din("c_prefsel", [8, 64])
    c_invc = din("c_invc", [128, 256])
    c_identf = din("c_identf", [128, 128])
    c_swap = din("c_swap", [128, 128])
    y = nc.dram_tensor("y", [T, D], F32, kind="ExternalOutput")
    XT = nc.dram_tensor("XT", [NG, D, 512], BF16)
    W1b = nc.dram_tensor("W1b", [8, 128, 8 * 512], BF16)
    W2b = nc.dram_tensor("W2b", [8, 128, 8 * 512], BF16)
    xown1 = nc.dram_tensor("xown1", [T, D], F32)
    agin = [nc.dram_tensor(f"agin{m}", [D, 512], BF16) for m in range(4)]
    agout = [nc.dram_tensor(f"agout{m}", [4 * D, 512], BF16) for m in range(4)]

    with contextlib.ExitStack() as st:
        sc = Sched(nc, st)

        def sb(name, shape, dt):
            return st.enter_context(nc.sbuf_tensor(name, list(shape), dt))

        ps = st.enter_context(nc.psum_tensor("ps", [128, 8, 512], F32))

        mixT = sb("mixT", [128, 8, T], BF16)
        identf = sb("identf", [128, 128], F32)
        swapm = sb("swapm", [128, 128], F32)
        halosel = sb("halosel", [128, 64], F32)
        prefsel = sb("prefsel", [8, 64], F32)
        invc = sb("invc", [128, 256], F32)
        negc = sb("negc", [128, NB, NH], F32)
        pscale = sb("pscale", [128, 4], F32)
        bft = sb("bft", [8, 1], F32)
        negbf = sb("negbf", [8, 1], F32)
        small = sb("small", [128, 64], F32)

        sc.dma("sp", "c_identf", identf[:], c_identf[:, :], writes=["identf"])
        sc.dma("sp", "c_swapm", swapm[:], c_swap[:, :], writes=["swapm"])
        sc.dma("sp", "c_halosel", halosel[:], c_halosel[:, :], writes=["halosel"])
        sc.dma("sp", "c_prefsel", prefsel[:], c_prefsel[:, :], writes=["prefsel"])
        sc.dma("sp", "c_invc", invc[:], c_invc[:, :], writes=["invc"])

        for l in range(L):
            layer(nc, sc, st, l, locals())

        sc.final_wait("sp", ["yout"] + ["dbg_" + k for k in dbgt])

        with nc.Block() as block:
            @block.tensor
            def _(e):
                sc.emit("pe", e)

            @block.scalar
            def _(e):
                sc.emit("act", e)

            @block.vector
            def _(e):
                sc.emit("dve", e)

            @block.gpsimd
            def _(e):
                sc.emit("pool", e)

            @block.sync
            def _(e):
                sc.emit("sp", e)
    return nc


def layer(nc, sc, st_outer, l, env):
    g = env
    ps = g["ps"]
    mixT, identf, swapm = g["mixT"], g["identf"], g["swapm"]
    halosel, prefsel, invc, negc = g["halosel"], g["prefsel"], g["invc"], g["negc"]
    pscale, bft, negbf, small = g["pscale"], g["bft"], g["negbf"], g["small"]
    xTfull, xownT, xown, XT, y = g["xTfull"], g["xownT"], g["xown"], g["XT"], g["y"]
    w_in, b_f, pool_w, pool_scale, w_out = g["w_in"], g["b_f"], g["pool_w"], g["pool_scale"], g["w_out"]
    ln1_g, ln1_b, w1, w2, ln2_g, ln2_b = g["ln1_g"], g["ln1_b"], g["w1"], g["w2"], g["ln2_g"], g["ln2_b"]
    c_mask = g["c_mask"]
    dbgt = g["dbgt"]
    L = f"L{l}_"
    nlayers = g["L"]
    last = (l == nlayers - 1)
    if l > 0:
        xown = g["xown1"]
    agin, agout, xown1 = g["agin"], g["agout"], g["xown1"]

    def dump(name, src, reads):
        if name in dbgt:
            sc.dma("sp", "dbg_" + name, dbgt[name].ap(), src, reads=reads, writes=["dbgout_" + name])

    def PS(b, n=1):
        if n == 1:
            return ps[:, b, :]
        return ps[:, b:b + n, :]

    def psk(b):
        return f"ps{b}"

    win_v = w_in[l].rearrange("(kc p) c -> p kc c", p=128)

    sc.dma("sp", "c_pscale", pscale[:], pool_scale[l].rearrange("(g p) -> p g", p=128), writes=["pscale"], slow=True)
    sc.dma("sp", "c_bft", bft[:], b_f[l].rearrange("(h o) -> h o", o=1), writes=["bft"])
    sc.op("dve", lambda e: e.tensor_scalar(out=negbf[:], in0=bft[:], scalar1=-1.0, scalar2=None, op0=ALU.mult),
          reads=["bft"], writes=["negbf"])

    XTv = XT.ap().rearrange("c (p kc) t -> c p kc t", kc=8)
    agov = [a.ap().rearrange("(r p kc) t -> r p kc t", r=4, kc=8) for a in agout]
    aginv = [a.ap().rearrange("(p kc) t -> p kc t", kc=8) for a in agin]
    W1b, W2b = g["W1b"], g["W2b"]

    def xt_chunk(c):
        if l == 0:
            return XTv[c]
        return agov[c // 4][c % 4]

    def xt_key(c):
        return f"XTc{c}" if l == 0 else "XT"

    with contextlib.ExitStack() as s1:
        def sb(name, shape, dt):
            return s1.enter_context(nc.sbuf_tensor(L + name, list(shape), dt))

        QT = sb("QT", [65, NH, T], BF16)
        xtail = sb("xtail", [128, 8, NG * 16], BF16)
        gs = sb("gs", [8, NG], F32)
        xTc = [sb(f"xTc{i}", [128, 8, 512], BF16) for i in range(2)]
        Wf = sb("Wf", [128, 8, 8], BF16)
        sc.dma("pool", "wf", Wf[:], win_v[:, :, 2048:2056], writes=["Wf"])
        sF = contextlib.ExitStack()
        lfT = sF.enter_context(nc.sbuf_tensor(L + "lfT", [8, S], F32))
        cumT = sF.enter_context(nc.sbuf_tensor(L + "cumT", [8, S], F32))
        zer = sF.enter_context(nc.sbuf_tensor(L + "zer", [8, 512], F32))
        sc.op("pool", lambda e: e.memset(zer[:], 0.0), writes=["zer"])

        def f_tail(c):
            sl = slice(c * 512, (c + 1) * 512)
            sc.op("act", lambda e, o=lfT[:, sl]: e.activation(out=o, in_=o, func=ACT.Ln, bias=1.0, scale=1.0),
                  reads=[f"lfT{c}"], writes=[f"lfT{c}"])
            sc.op("dve", lambda e, o=gs[:, c:c + 1], i=lfT[:, sl]: e.tensor_reduce(out=o, in_=i, axis=AX.X, op=ALU.add),
                  reads=[f"lfT{c}"], writes=["gs"])
            init = 0.0 if c == 0 else cumT[:, c * 512 - 1:c * 512]
            sc.op("dve", lambda e, o=cumT[:, sl], d0=lfT[:, sl], ini=init:
                  e.tensor_tensor_scan(out=o, data0=d0, data1=zer[:], initial=ini, op0=ALU.add, op1=ALU.add),
                  reads=[f"lfT{c}", "zer", "cumT"], writes=["cumT"])

        with contextlib.ExitStack() as sA:
            xin = [sA.enter_context(nc.sbuf_tensor(L + f"pxin{i}", [128, 4, D], F32)) for i in range(1)] * 2
            xts = [sA.enter_context(nc.sbuf_tensor(L + f"pxts{i}", [128, 8, 512], BF16)) for i in range(2)]
            Wq = sA.enter_context(nc.sbuf_tensor(L + "Wq", [128, 8, 512], BF16))
            sc.dma("pool", "wq", Wq[:], win_v[:, :, 512:1024], writes=["Wq"])

            def loadX(m):
                sc.dma("sp", "pxin", xin[0][:], xown[m * 512:(m + 1) * 512, :].rearrange("(b p) d -> p b d", p=128),
                       reads=["xown1"], writes=["pxin"])
            xoT = xownT.ap().rearrange("(kc p) t -> p kc t", p=128)
            xfT = xTfull.ap().rearrange("(kc p) t -> p kc t", p=128)

            def loadOwn(m):
                sc.dma("pool", f"xo{m % 2}", xts[m % 2][:], xoT[:, :, m * 512:(m + 1) * 512], writes=[f"xts{m % 2}"])
            if l == 0:
                loadOwn(0)
                loadOwn(1)
                for c in range(NG):
                    sc.dma("pool", f"xc{c}", XTv[c], xfT[:, :, c * 512:(c + 1) * 512], writes=[f"XTc{c}"])
            for m in range(4):
                s_ = m % 2
                if l > 0:
                    if m == 0:
                        sc.dma("sp", f"xts{s_}", xts[s_][:], aginv[m], reads=[f"agin{m}"], writes=[f"xts{s_}"])
                    if m + 1 < 4:
                        sc.dma("sp", f"xts{(m + 1) % 2}", xts[(m + 1) % 2][:], aginv[m + 1], reads=[f"agin{m + 1}"], writes=[f"xts{(m + 1) % 2}"])
                for blk in ():
                    pb = (blk % 2) * 2
                    for kc in range(8):
                        sc.op("pe", lambda e, o=ps[:, pb + kc // 4, (kc % 4) * 128:(kc % 4 + 1) * 128],
                              i=xin[0][:, blk, kc * 128:(kc + 1) * 128]: e.transpose(out=o, in_=i, identity=identf[:]),
                              reads=["pxin", "identf"], writes=[psk(pb + kc // 4)])
                    src = ps[:, pb:pb + 2, :].rearrange("p a (k t) -> p (a k) t", t=128)
                    dst = xts[s_][:, :, blk * 128:(blk + 1) * 128]
                    if blk % 2 == 0:
                        sc.op("act", lambda e, o=dst, i=src: e.activation(out=o, in_=i, func=ACT.Copy),
                              reads=[psk(pb), psk(pb + 1)], writes=[f"xts{s_}"])
                    else:
                        sc.op("dve", lambda e, o=dst, i=src: e.tensor_copy(out=o, in_=i),
                              reads=[psk(pb), psk(pb + 1)], writes=[f"xts{s_}"])
                if l == 0:
                    sc.dma("sp", f"xts{s_}", aginv[m], xts[s_][:], reads=[f"xts{s_}"], writes=[f"agin{m}"])
                if False:
                    sc.collective("cc", lambda e, m=m: e.collective_compute("AllGather", ALU.bypass, replica_groups=[[0, 1, 2, 3], [4, 5, 6, 7]],
                                                                            ins=[agin[m].ap().opt()], outs=[agout[m].ap().opt()]),
                                  reads=[f"agin{m}"], writes=["XT"])
                for h in range(NH):
                    pb = 4 + h % 2
                    for kc in range(8):
                        sc.op("pe", lambda e, o=ps[0:64, pb, :], a=Wq[:, kc, h * 64:(h + 1) * 64], b=xts[s_][:, kc, :], k=kc:
                              e.matmul(o, lhsT=a, rhs=b, start=(k == 0), stop=(k == 7)),
                              reads=["Wq", f"xts{s_}"], writes=[psk(pb)])
                    sc.op("act", lambda e, o=QT[0:64, h, m * 512:(m + 1) * 512], i=ps[0:64, pb, :]:
                          e.activation(out=o, in_=i, func=ACT.Copy, scale=0.125),
                          reads=[psk(pb)], writes=[f"QT{h}"])
                if l == 0 and m + 2 < 4:
                    loadOwn(m + 2)
        sc.barrier()
        with contextlib.ExitStack() as sC:
            def loadC(c):
                sc.dma("sp", f"xTc{c % 2}", xTc[c % 2][:], xt_chunk(c), reads=[xt_key(c)], writes=[f"xTc{c % 2}"])
            loadC(0)
            for c in range(NG):
                s_ = c % 2
                if c + 1 < NG:
                    loadC(c + 1)
                pb = c % 2
                sc.op("dve", lambda e, o=xtail[:, :, c * 16:(c + 1) * 16], i=xTc[s_][:, :, 496:512]: e.tensor_copy(out=o, in_=i),
                      reads=[f"xTc{s_}"], writes=["xtail"])
                for kc in range(8):
                    sc.op("pe", lambda e, o=ps[0:8, pb, :], a=Wf[:, kc, :], b=xTc[s_][:, kc, :], k=kc:
                          e.matmul(o, lhsT=a, rhs=b, start=(k == 0), stop=(k == 7)),
                          reads=["Wf", f"xTc{s_}"], writes=[psk(pb)])
                sc.op("act", lambda e, o=lfT[:, c * 512:(c + 1) * 512], i=ps[0:8, pb, :]:
                      e.activation(out=o, in_=i, func=ACT.Exp, bias=negbf[:], scale=-1.0),
                      reads=[psk(pb), "negbf"], writes=[f"lfT{c}"])
                f_tail(c)
            for blk in range(NB):
                sc.op("pe", lambda e, o=ps[:, 2, blk * 8:(blk + 1) * 8], i=cumT[:, blk * 128:(blk + 1) * 128]:
                      e.transpose(out=o, in_=i, identity=identf[0:8, 0:8]),
                      reads=["cumT", "identf"], writes=[psk(2)])
            sc.op("dve", lambda e: e.tensor_copy(out=negc[:].rearrange("p b h -> p (b h)"), in_=ps[:, 2, :]),
                  reads=[psk(2)], writes=["negc"])

        sc.barrier()
        sF.close()
        mask = sb("mask", [128, 16, 512], BF16)
        Wk = [sb(f"Wk{i}", [128, 8, 128], BF16) for i in range(2)]
        Wv = [sb(f"Wv{i}", [128, 8, 128], BF16) for i in range(2)]
        sc.dma("sp", "c_mask", mask[:], c_mask[:, :].rearrange("p (j q) -> p j q", q=512), writes=["mask"])
        for w_ in range(1):
            sc.dma("pool", f"wk{w_}", Wk[w_][:], win_v[:, :, 1024:1024 + 128], writes=[f"Wk{w_}"])
            sc.dma("pool", f"wv{w_}", Wv[w_][:], win_v[:, :, 1536:1536 + 128], writes=[f"Wv{w_}"])
        sc.dma("sp", "xTc0", xTc[0][:], xt_chunk(0), reads=[xt_key(0)], writes=["xTc0"])
        Wu = sb("Wu", [128, 8, 512], BF16)
        pw = sb("pw", [128, 4, 128], BF16)
        sc.dma("pool", "wu", Wu[:], win_v[:, :, 0:512], writes=["Wu"])
        sc.dma("pool", "pw", pw[:], pool_w[l].rearrange("g c d -> c g d"), writes=["pw"])
        dump("d_negc", negc[:].rearrange("p b h -> p (b h)"), ["negc"])
        with contextlib.ExitStack() as sB:
            def sbB(name, shape, dt):
                return sB.enter_context(nc.sbuf_tensor(L + name, list(shape), dt))
            xTo = [sbB(f"xTo{i}", [128, 8, 512], BF16) for i in range(2)]
            uT = sbB("uT", [128, 4, 528], F32)
            s2 = sbB("s2", [128, 4, 528], F32)
            s4 = sbB("s4", [128, 4, 528], F32)
            utail = sbB("utail", [128, 4, NG * 16], F32)
            htmp = sbB("htmp", [128, 4, 16, 16], F32)
            diffT = sbB("diffT", [128, 4, 512], BF16)
            dtmp = sbB("dtmp", [128, 4, 16], F32)
            lfo = sbB("lfo", [8, T], F32)
            cumo = sbB("cumo", [8, T], F32)
            aq = sbB("aq", [8, T], BF16)
            pref = sbB("pref", [8, 4], F32)
            ptmp = sbB("ptmp", [8, NG], F32)
            zerB = sbB("zerB", [8, 512], F32)
            sc.op("pool", lambda e: e.memset(zerB[:], 0.0), writes=["zerB"])

            for gq in range(4):
                for kc in range(8):
                    sc.op("pe", lambda e, o=ps[:, 3, 0:256], a=Wu[:, kc, gq * 128:(gq + 1) * 128], b=xtail[:, kc, :], k=kc:
                          e.matmul(o, lhsT=a, rhs=b, start=(k == 0), stop=(k == 7)),
                          reads=["Wu", "xtail"], writes=[psk(3)])
                sc.op("dve", lambda e, o=utail[:, gq, :], i=ps[:, 3, 0:256]: e.tensor_copy(out=o, in_=i),
                      reads=[psk(3)], writes=["utail"])

            for m in range(4):
                s_ = m % 2
                if m == 0:
                    sc.dma("sp", f"xTo{s_}", xTo[s_][:], aginv[m], reads=[f"agin{m}"], writes=[f"xTo{s_}"])
                if m + 1 < 4:
                    sc.dma("sp", f"xTo{(m + 1) % 2}", xTo[(m + 1) % 2][:], aginv[m + 1], reads=[f"agin{m + 1}"], writes=[f"xTo{(m + 1) % 2}"])
                for kc in range(8):
                    sc.op("pe", lambda e, o=ps[0:8, 6, :], a=Wf[:, kc, :], b=xTo[s_][:, kc, :], k=kc:
                          e.matmul(o, lhsT=a, rhs=b, start=(k == 0), stop=(k == 7)),
                          reads=["Wf", f"xTo{s_}"], writes=[psk(6)])
                sc.op("act", lambda e, o=lfo[:, m * 512:(m + 1) * 512], i=ps[0:8, 6, :]:
                      e.activation(out=o, in_=i, func=ACT.Exp, bias=negbf[:], scale=-1.0),
                      reads=[psk(6), "negbf"], writes=["lfo"])
                sc.op("act", lambda e, o=lfo[:, m * 512:(m + 1) * 512]:
                      e.activation(out=o, in_=o, func=ACT.Ln, bias=1.0, scale=1.0),
                      reads=["lfo"], writes=["lfo"])
                sc.op("dve", lambda e, i1=prefsel[:, m * 16:(m + 1) * 16]: e.tensor_tensor(out=ptmp[:], in0=gs[:], in1=i1, op=ALU.mult),
                      reads=["gs", "prefsel"], writes=["ptmp"])
                sc.op("dve", lambda e, o=pref[:, m:m + 1]: e.tensor_reduce(out=o, in_=ptmp[:], axis=AX.X, op=ALU.add),
                      reads=["ptmp"], writes=["pref"])
                sc.op("dve", lambda e, o=cumo[:, m * 512:(m + 1) * 512], d0=lfo[:, m * 512:(m + 1) * 512], ini=pref[:, m:m + 1]:
                      e.tensor_tensor_scan(out=o, data0=d0, data1=zerB[:], initial=ini, op0=ALU.add, op1=ALU.add),
                      reads=["lfo", "zerB", "pref"], writes=["cumo"])
                sc.op("dve", lambda e, o=aq[:, m * 512:(m + 1) * 512], i=cumo[:, m * 512:(m + 1) * 512]:
                      e.tensor_scalar(out=o, in0=i, scalar1=-1.0, scalar2=None, op0=ALU.mult),
                      reads=["cumo"], writes=["aq"])

                for gq in range(4):
                    pb = 4 + gq % 2
                    for kc in range(8):
                        sc.op("pe", lambda e, o=ps[:, pb, :], a=Wu[:, kc, gq * 128:(gq + 1) * 128], b=xTo[s_][:, kc, :], k=kc:
                              e.matmul(o, lhsT=a, rhs=b, start=(k == 0), stop=(k == 7)),
                              reads=["Wu", f"xTo{s_}"], writes=[psk(pb)])
                    sc.op("act", lambda e, o=uT[:, gq, 16:528], i=ps[:, pb, :]: e.activation(out=o, in_=i, func=ACT.Copy),
                          reads=[psk(pb)], writes=["uT"])
                sc.op("dve", lambda e, sel=halosel[:, m * 16:(m + 1) * 16].unsqueeze(1).unsqueeze(1).to_broadcast([128, 4, 16, 16]):
                      e.tensor_tensor(out=htmp[:], in0=utail[:].rearrange("p g (G t) -> p g t G", t=16), in1=sel, op=ALU.mult),
                      reads=["utail", "halosel"], writes=["htmp"])
                sc.op("dve", lambda e: e.tensor_reduce(out=uT[:, :, 0:16], in_=htmp[:], axis=AX.X, op=ALU.add),
                      reads=["htmp", "uT"], writes=["uT"])
                sc.op("dve", lambda e: e.tensor_tensor(out=s2[:, :, 1:528], in0=uT[:, :, 1:528], in1=uT[:, :, 0:527], op=ALU.add),
                      reads=["uT"], writes=["s2"])
                sc.op("dve", lambda e: e.tensor_tensor(out=s4[:, 1:4, 3:528], in0=s2[:, 1:4, 3:528], in1=s2[:, 1:4, 1:526], op=ALU.add),
                      reads=["s2"], writes=["s4"])
                def diff(gq, ssum, w, m=m):
                    sc.op("dve", lambda e: e.scalar_tensor_tensor(out=diffT[:, gq, 16:512], in0=ssum[:, gq, 32:528], scalar=1.0 / w,
                                                                  in1=uT[:, gq, 32:528], op0=ALU.mult, op1=ALU.subtract),
                          reads=["s2", "s4", "uT"], writes=["diffT"])
                    sc.op("dve", lambda e: e.tensor_tensor(out=dtmp[:, gq, :], in0=ssum[:, gq, 16:32],
                                                           in1=invc[:, (m * 4 + gq) * 16:(m * 4 + gq + 1) * 16], op=ALU.mult),
                          reads=["s2", "s4", "invc"], writes=["dtmp"])
                    sc.op("dve", lambda e: e.tensor_tensor(out=diffT[:, gq, 0:16], in0=dtmp[:, gq, :], in1=uT[:, gq, 16:32], op=ALU.subtract),
                          reads=["dtmp", "uT"], writes=["diffT"])
                diff(0, s2, 2)
                diff(1, s4, 4)
                sc.op("dve", lambda e: e.tensor_tensor(out=s2[:, 2:4, 7:528], in0=s4[:, 2:4, 7:528], in1=s4[:, 2:4, 3:524], op=ALU.add),
                      reads=["s4", "diffT", "dtmp"], writes=["s2"])
                diff(2, s2, 8)
                sc.op("dve", lambda e: e.tensor_tensor(out=s4[:, 3:4, 15:528], in0=s2[:, 3:4, 15:528], in1=s2[:, 3:4, 7:520], op=ALU.add),
                      reads=["s2", "diffT", "dtmp"], writes=["s4"])
                diff(3, s4, 16)
                for gq in range(4):
                    pb = 4 + gq % 2
                    sc.op("pe", lambda e, o=ps[:, pb, :], a=pw[:, gq, :], b=diffT[:, gq, :]: e.matmul(o, lhsT=a, rhs=b, start=True, stop=True),
                          reads=["pw", "diffT"], writes=[psk(pb)])
                    sc.op("act", lambda e, o=mixT[:, gq, m * 512:(m + 1) * 512], i=ps[:, pb, :], s=pscale[:, gq:gq + 1]:
                          e.activation(out=o, in_=i, func=ACT.Copy, scale=s),
                          reads=[psk(pb), "pscale"], writes=[f"mixT{gq}"])
            for h in range(NH):
                sc.dma("sp", "aqmv", QT[64:65, h, :], aq[h:h + 1, :], reads=["aq"], writes=["QTaug"])
            dump("d_QT", QT[:].rearrange("p h t -> p (h t)"), [f"QT{h}" for h in range(NH)] + ["QTaug"])
            dump("d_lfo", cumo[:], ["cumo"])
            dump("d_uT", uT[:].rearrange("p g t -> p (g t)"), ["uT"])
            dump("d_diffT", diffT[:].rearrange("p g t -> p (g t)"), ["diffT"])

        sc.barrier()
        with contextlib.ExitStack() as sT:
            def sbT(name, shape, dt):
                return sT.enter_context(nc.sbuf_tensor(L + name, list(shape), dt))
            KT = sbT("KT", [65, 2, S], BF16)
            Vaug = sbT("Vaug", [128, NB, 192], BF16)
            Pt = [sbT(f"Pt{i}", [128, 512], BF16) for i in range(3)]
            RA = sbT("RA", [128, 512], F32)
            RB = sbT("RB", [128, 512], F32)
            Wsb = sbT("Wsb", [128, 512], F32)
            ktmp = [sbT(f"ktmp{i}", [128, 512], BF16) for i in range(2)]
            sc.op("pool", lambda e: e.memset(KT[64:65, :, :], 1.0), writes=["KTaug"])
            sc.op("pool", lambda e: e.memset(Vaug[:, :, 64:128], 1.0), writes=["Vones"])

            def loadW(hp_):
                w_ = hp_ % 2
                sc.dma("pool", f"wk{w_}", Wk[w_][:], win_v[:, :, 1024 + hp_ * 128:1024 + (hp_ + 1) * 128], writes=[f"Wk{w_}"])
                sc.dma("pool", f"wv{w_}", Wv[w_][:], win_v[:, :, 1536 + hp_ * 128:1536 + (hp_ + 1) * 128], writes=[f"Wv{w_}"])

            xTcA = [xTc[0], xTc[1], sbT("xTc2", [128, 8, 512], BF16), sbT("xTc3", [128, 8, 512], BF16)]

            def loadK(c):
                sc.dma("sp", f"xTc{c % 4}", xTcA[c % 4][:], xt_chunk(c), reads=[xt_key(c)], writes=[f"xTc{c % 4}"])
            for c_ in (1, 2, 3):
                loadK(c_)
            w1v_ = w1[l].rearrange("(kc p) c -> p kc c", p=128)
            w2v_ = w2[l].rearrange("(f p) c -> p f c", p=128)
            casts = []
            for s_i in range(8):
                casts.append(("wcast1", W1b.ap()[s_i].rearrange("p (k c) -> p k c", c=512), w1v_[:, :, s_i * 512:(s_i + 1) * 512], "W1b"))
            for half in range(2):
                for sc_ in range(4):
                    casts.append(("wcast2", W2b.ap()[half * 4 + sc_].rearrange("p (k c) -> p k c", c=512),
                                  w2v_[:, sc_ * 8:(sc_ + 1) * 8, half * 512:(half + 1) * 512], "W2b"))
            pending = [None]
            for hp in range(4):
                ws = hp % 2
                for c in range(NG):
                    s_ = c % 4
                    pb = 6 + (c % 2)
                    kt_ = c % 2
                    for kc in range(8):
                        sc.op("pe", lambda e, o=ps[:, pb, :], a=Wk[ws][:, kc, :], b=xTcA[s_][:, kc, :], k=kc:
                              e.matmul(o, lhsT=a, rhs=b, start=(k == 0), stop=(k == 7)),
                              reads=[f"Wk{ws}", f"xTc{s_}"], writes=[psk(pb)])
                    sc.op("act", lambda e, o=KT[0:64, 0, c * 512:(c + 1) * 512], i=ps[0:64, pb, :]:
                          e.activation(out=o, in_=i, func=ACT.Copy),
                          reads=[psk(pb)], writes=["KT0"])
                    sc.op("act", lambda e, o=ktmp[kt_][64:128, :], i=ps[64:128, pb, :]: e.activation(out=o, in_=i, func=ACT.Copy),
                          reads=[psk(pb)], writes=[f"ktmp{kt_}"])
                    sc.dma("sp", f"ktB{kt_}", KT[0:64, 1, c * 512:(c + 1) * 512], ktmp[kt_][64:128, :],
                           reads=[f"ktmp{kt_}"], writes=[f"KT1_{kt_}"])
                    pb = 5
                    for blk in range(4):
                        for kc in range(8):
                            sc.op("pe", lambda e, o=ps[:, pb, blk * 128:(blk + 1) * 128], a=xTcA[s_][:, kc, blk * 128:(blk + 1) * 128],
                                  b=Wv[ws][:, kc, :], k=kc: e.matmul(o, lhsT=a, rhs=b, start=(k == 0), stop=(k == 7)),
                                  reads=[f"Wv{ws}", f"xTc{s_}"], writes=[psk(pb)])
                    pv = ps[:, pb, :].rearrange("p (b c) -> p b c", c=128)
                    sc.op("dve", lambda e, o=Vaug[:, c * 4:(c + 1) * 4, 0:64], i=pv[:, :, 0:64]: e.tensor_copy(out=o, in_=i),
                          reads=[psk(pb)], writes=["VA"])
                    sc.op("dve", lambda e, o=Vaug[:, c * 4:(c + 1) * 4, 128:192], i=pv[:, :, 64:128]: e.tensor_copy(out=o, in_=i),
                          reads=[psk(pb)], writes=["VB"])
                    if c + 4 < NG:
                        loadK(c + 4)

                if hp + 1 < 4:
                    loadW(hp + 1)
                    for c_ in range(4):
                        loadK(c_)
                for sem_, dst_, src_, key_ in casts[hp * 4:(hp + 1) * 4]:
                    sc.dma("pool", sem_, dst_, src_, reads=["VB", "KT0"], writes=[key_])
                if hp == 0:
                    dump("d_KT", KT[:].rearrange("p h t -> p (h t)"), ["KT0", "KT1_0", "KT1_1", "KTaug"])
                    dump("d_V", Vaug[:].rearrange("p b c -> p (b c)"), ["VA", "VB", "Vones"])
                for hh in range(2):
                    h = 2 * hp + hh
                    vk = "VA" if hh == 0 else "VB"
                    for m in range(4):
                        NJ = 16 * m + 16
                        ob = 3 + (m % 2)
                        q_ap = QT[0:65, h, m * 512:(m + 1) * 512]

                        def qk(j):
                            sb_ = j % 3
                            sc.op("pe", lambda e, o=ps[:, sb_, :], a=KT[0:65, hh, j * 128:(j + 1) * 128], q_ap=q_ap:
                                  e.matmul(o, lhsT=a, rhs=q_ap, start=True, stop=True),
                                  reads=(["KT0"] if hh == 0 else ["KT1_0", "KT1_1"]) + ["KTaug", f"QT{h}", "QTaug"], writes=[psk(sb_)])
                        qk(0)
                        qk(1)
                        for j in range(NJ):
                            if j == 6 and pending[0] is not None:
                                pending[0]()
                                pending[0] = None
                            if j + 2 < NJ:
                                qk(j + 2)
                            sb_ = j % 3
                            sc.op("act", lambda e, o=Pt[sb_][:], i=ps[:, sb_, :], bia=negc[:, j, h:h + 1]:
                                  e.activation(out=o, in_=i, func=ACT.Exp, bias=bia, scale=1.0),
                                  reads=[psk(sb_), "negc"], writes=[f"Pt{sb_}"])
                            if j >= 16 * m:
                                sc.op("dve", lambda e, o=Pt[sb_][:], mk=mask[:, j - 16 * m, :]:
                                      e.tensor_tensor(out=o, in0=o, in1=mk, op=ALU.min),
                                      reads=[f"Pt{sb_}", "mask"], writes=[f"Pt{sb_}"])
                            sc.op("pe", lambda e, o=ps[:, ob, :], a=Vaug[:, j, hh * 64:hh * 64 + 128], b=Pt[sb_][:], jj=j, NJ=NJ:
                                  e.matmul(o, lhsT=a, rhs=b, start=(jj == 0), stop=(jj == NJ - 1)),
                                  reads=[vk, "Vones", f"Pt{sb_}"], writes=[psk(ob)])
                        if hh == 0:
                            R, sums, outs = RA, slice(64, 128), slice(0, 64)
                        else:
                            R, sums, outs = RB, slice(0, 64), slice(64, 128)
                        rk = "RA" if hh == 0 else "RB"
                        sc.op("dve", lambda e, o=R[sums, :], i=ps[sums, ob, :]: e.reciprocal(out=o, in_=i),
                              reads=[psk(ob), rk], writes=[rk])
                        sc.dma("sp", "nrm", Wsb[outs, :], R[sums, :], reads=[rk], writes=["Wsb"])

                        def fin(o=mixT[outs, 4 + hp, m * 512:(m + 1) * 512], i0=ps[outs, ob, :], i1=Wsb[outs, :], ob=ob, key=f"mixT{4 + hp}_{hh}"):
                            sc.op("dve", lambda e: e.tensor_tensor(out=o, in0=i0, in1=i1, op=ALU.mult),
                                  reads=[psk(ob), "Wsb"], writes=[key])
                        pending[0] = fin
            if pending[0] is not None:
                pending[0]()
                pending[0] = None

    sc.barrier()
    with contextlib.ExitStack() as s2_:
        def sb2(name, shape, dt):
            return s2_.enter_context(nc.sbuf_tensor(L + name, list(shape), dt))
        xres = sb2("xres", [128, 16, D], F32)
        zt = sb2("zt", [128, 4, D], F32)
        xn = sb2("xn", [128, D], F32)
        stt = sb2("stt", [128, 12], F32)
        gb = [None] * 4
        sD = contextlib.ExitStack()
        Wo = sD.enter_context(nc.sbuf_tensor(L + "Wo", [128, 8, D], BF16))
        gb[0] = sD.enter_context(nc.sbuf_tensor(L + "gb0", [128, D], F32))
        gb[1] = sD.enter_context(nc.sbuf_tensor(L + "gb1", [128, D], F32))

        sc.dma("sp", "xres", xres[:], xown.ap().rearrange("(lb p) d -> p lb d", p=128), reads=["xown1"], writes=["xres"])
        sc.dma("pool", "wo", Wo[:], w_out[l].rearrange("(mc p) c -> p mc c", p=128), writes=["Wo"])
        for i, v in enumerate((ln1_g, ln1_b)):
            sc.dma("sp", f"gb{i}", gb[i][:], v[l].partition_broadcast(128), writes=[f"gb{i}"])

        mixkeys = [f"mixT{i}" for i in range(4)] + [f"mixT{4 + hp}_{hh}" for hp in range(4) for hh in range(2)]

        def ln_steps(zsrc, zkey, gi, bi, lb, final):
            st_ = []
            st_.append(lambda: sc.op("dve", lambda e: e.bn_stats(out=stt[:, 0:6], in_=zsrc[:, 0:512]), reads=[zkey], writes=["stt"]))
            st_.append(lambda: sc.op("dve", lambda e: e.bn_stats(out=stt[:, 6:12], in_=zsrc[:, 512:1024]), reads=[zkey, "stt"], writes=["stt"]))
            st_.append(lambda: sc.op("dve", lambda e: e.bn_aggr(out=small[:, 0:2], in_=stt[:, :]), reads=["stt"], writes=["small"]))
            st_.append(lambda: sc.op("act", lambda e: e.activation(out=small[:, 2:3], in_=small[:, 1:2], func=ACT.Sqrt, bias=small[:, 8:9], scale=1.0),
                                     reads=["small", "epsc"], writes=["small_sd"]))
            st_.append(lambda: sc.op("dve", lambda e: e.reciprocal(out=small[:, 3:4], in_=small[:, 2:3]), reads=["small_sd"], writes=["small_r"]))
            st_.append(lambda: sc.op("dve", lambda e: e.tensor_scalar(out=small[:, 4:5], in0=small[:, 0:1], scalar1=small[:, 3:4], scalar2=-1.0,
                                                                      op0=ALU.mult, op1=ALU.mult),
                                     reads=["small", "small_r"], writes=["small_nm"]))
            st_.append(lambda: sc.op("act", lambda e: e.activation(out=xn[:], in_=zsrc, func=ACT.Identity, scale=small[:, 3:4], bias=small[:, 4:5]),
                                     reads=[zkey, "small_r", "small_nm"], writes=["xn"]))
            st_.append(lambda: sc.op("pool", lambda e: e.tensor_tensor(out=xn[:], in0=xn[:], in1=gb[gi][:], op=ALU.mult),
                                     reads=["xn", f"gb{gi}"], writes=["xn"]))

            def last_():
                sc.op("pool", lambda e: e.tensor_tensor(out=xres[:, lb, :], in0=xn[:], in1=gb[bi][:], op=ALU.add),
                      reads=["xn", f"gb{bi}", f"xres{lb}", "xres"], writes=[f"xres{lb}"])
                if final and last:
                    sc.dma("sp", "yout", y[lb * 128:(lb + 1) * 128, :], xres[:, lb, :], reads=[f"xres{lb}"], writes=[f"y{lb}"])
                elif final:
                    sc.dma("sp", "x1out", xown1[lb * 128:(lb + 1) * 128, :], xres[:, lb, :], reads=[f"xres{lb}"], writes=["xown1"])
            st_.append(last_)
            return st_

        def layernorm(zsrc, zkey, gi, bi, lb, final):
            for f_ in ln_steps(zsrc, zkey, gi, bi, lb, final):
                f_()

        sc.op("pool", lambda e: e.memset(small[:, 8:9], LN_EPS), writes=["epsc"])
        dump("d_mixT", mixT[:].rearrange("p c t -> p (c t)"), mixkeys)

        for lb in range(16):
            for half in range(2):
                pb = (lb % 2) * 2 + half
                for mc in range(8):
                    sc.op("pe", lambda e, o=ps[:, pb, :], a=mixT[:, mc, lb * 128:(lb + 1) * 128], b=Wo[:, mc, half * 512:(half + 1) * 512], k=mc:
                          e.matmul(o, lhsT=a, rhs=b, start=(k == 0), stop=(k == 7)),
                          reads=mixkeys + ["Wo"], writes=[psk(pb)])
                sc.op("dve", lambda e, o=zt[:, lb % 2, half * 512:(half + 1) * 512], i0=xres[:, lb, half * 512:(half + 1) * 512], i1=ps[:, pb, :]:
                      e.scalar_tensor_tensor(out=o, in0=i0, scalar=ALPHA, in1=i1, op0=ALU.mult, op1=ALU.add),
                      reads=["xres", f"xres{lb}", psk(pb)], writes=[f"zt{lb % 2}"])
            layernorm(zt[:, lb % 2, :], f"zt{lb % 2}", 0, 1, lb, False)

        dump("d_x1", xres[:].rearrange("p b d -> p (b d)"), [f"xres{lb}" for lb in range(16)])
        sc.barrier()
        sD.close()
        gb[2] = sb2("gb2", [128, D], F32)
        gb[3] = sb2("gb3", [128, D], F32)
        x1T = sb2("x1T", [128, 8, 512], BF16)
        hidT = sb2("hidT", [128, 32, 512], BF16)
        rtmp = [sb2(f"rtmp{i}", [128, 512], F32) for i in range(2)]
        NSLOT = 4
        wsl = [sb2(f"wsl{i}", [128, 8, 512], BF16) for i in range(NSLOT)]
        for i, v in ((2, ln2_g), (3, ln2_b)):
            sc.dma("sp", f"gb{i}", gb[i][:], v[l].partition_broadcast(128), writes=[f"gb{i}"])
        w1v = w1[l].rearrange("(kc p) c -> p kc c", p=128)
        w2v = w2[l].rearrange("(f p) c -> p f c", p=128)
        slot_ctr = [0]

        def wload(src):
            i = slot_ctr[0] % NSLOT
            slot_ctr[0] += 1
            sc.dma("sp", f"wsl{i}", wsl[i][:], src[0], reads=[src[1]], writes=[f"wsl{i}"])
            return i

        wsrcs = []
        for m in range(4):
            for s_ in range(8):
                wsrcs.append((W1b.ap()[s_].rearrange("p (k c) -> p k c", c=512), "W1b"))
            for half in range(2):
                for sc_ in range(4):
                    wsrcs.append((W2b.ap()[half * 4 + sc_].rearrange("p (k c) -> p k c", c=512), "W2b"))
        issued = [0]
        used = [0]

        def wnext():
            while issued[0] < len(wsrcs) and issued[0] < used[0] + NSLOT - 1:
                wload(wsrcs[issued[0]])
                issued[0] += 1
            i = used[0] % NSLOT
            used[0] += 1
            return i

        def x1T_build(m):
            for blk in range(4):
                lb = m * 4 + blk
                pb = 6
                for kc in range(8):
                    sc.op("pe", lambda e, o=ps[:, pb + kc // 4, (kc % 4) * 128:(kc % 4 + 1) * 128],
                          i=xres[:, lb, kc * 128:(kc + 1) * 128]: e.transpose(out=o, in_=i, identity=identf[:]),
                          reads=[f"xres{lb}", "identf"], writes=[psk(pb + kc // 4)])
                src = ps[:, pb:pb + 2, :].rearrange("p a (k t) -> p (a k) t", t=128)
                sc.op("act", lambda e, o=x1T[:, :, blk * 128:(blk + 1) * 128], i=src: e.activation(out=o, in_=i, func=ACT.Copy),
                      reads=[psk(pb), psk(pb + 1)], writes=["x1T"])

        def x2T_gather(g_):
            for blk in range(4):
                lb = g_ * 4 + blk
                pb = 6
                for kc in range(8):
                    sc.op("pe", lambda e, o=ps[:, pb + kc // 4, (kc % 4) * 128:(kc % 4 + 1) * 128],
                          i=xres[:, lb, kc * 128:(kc + 1) * 128]: e.transpose(out=o, in_=i, identity=identf[:]),
                          reads=[f"xres{lb}", "identf"], writes=[psk(pb + kc // 4)])
                src = ps[:, pb:pb + 2, :].rearrange("p a (k t) -> p (a k) t", t=128)
                sc.op("act", lambda e, o=x1T[:, :, blk * 128:(blk + 1) * 128], i=src: e.activation(out=o, in_=i, func=ACT.Copy),
                      reads=[psk(pb), psk(pb + 1)], writes=["x1T"])
            sc.dma("sp", "x2T", aginv[g_], x1T[:], reads=["x1T"], writes=[f"agin{g_}"])
            sc.collective("cc", lambda e, g_=g_: e.collective_compute("AllGather", ALU.bypass, replica_groups=[[0, 1, 2, 3], [4, 5, 6, 7]],
                                                                       ins=[agin[g_].ap().opt()], outs=[agout[g_].ap().opt()], dma_qos="P3"),
                          reads=[f"agin{g_}"], writes=["XT"])

        pend = []
        x1T_build(0)
        for m in range(4):
            for s_ in range(8):
                si = wnext()
                for q in range(4):
                    fc = s_ * 4 + q
                    pb = 4 + fc % 2
                    for kc in range(8):
                        sc.op("pe", lambda e, o=ps[:, pb, :], a=wsl[si][:, kc, q * 128:(q + 1) * 128], b=x1T[:, kc, :], k=kc:
                              e.matmul(o, lhsT=a, rhs=b, start=(k == 0), stop=(k == 7)),
                              reads=[f"wsl{si}", "x1T"], writes=[psk(pb)])
                    rt = rtmp[fc % 2]
                    sc.op("act", lambda e, o=rt[:], i=ps[:, pb, :]: e.activation(out=o, in_=i, func=ACT.Relu),
                          reads=[psk(pb)], writes=[f"rtmp{fc % 2}"])
                    sc.op("dve", lambda e, o=hidT[:, fc, :], i=rt[:]: e.tensor_tensor(out=o, in0=i, in1=i, op=ALU.mult),
                          reads=[f"rtmp{fc % 2}"], writes=[f"hid{fc}"])
                    for _ in range(2):
                        if pend:
                            pend.pop(0)()
            while pend:
                pend.pop(0)()
            if not last and m >= 1:
                x2T_gather(m - 1)
            hidkeys = [f"hid{fc}" for fc in range(32)]
            for half in range(2):
                for sc_ in range(4):
                    si = wnext()
                    for blk in range(4):
                        for fq in range(8):
                            fc = sc_ * 8 + fq
                            sc.op("pe", lambda e, o=ps[:, blk, :], a=hidT[:, fc, blk * 128:(blk + 1) * 128], b=wsl[si][:, fq, :], k=fc:
                                  e.matmul(o, lhsT=a, rhs=b, start=(k == 0), stop=(k == 31)),
                                  reads=hidkeys + [f"wsl{si}"], writes=[psk(blk)])
                for blk in range(4):
                    lb = m * 4 + blk
                    sc.op("dve", lambda e, o=zt[:, blk, half * 512:(half + 1) * 512], i0=xres[:, lb, half * 512:(half + 1) * 512], i1=ps[:, blk, :]:
                          e.scalar_tensor_tensor(out=o, in0=i0, scalar=ALPHA, in1=i1, op0=ALU.mult, op1=ALU.add),
                          reads=[f"xres{lb}", psk(blk)], writes=[f"zt{blk}"])
            if m + 1 < 4:
                x1T_build(m + 1)
            for blk in range(4):
                pend += ln_steps(zt[:, blk, :], f"zt{blk}", 2, 3, m * 4 + blk, True)
        while pend:
            pend.pop(0)()
        if not last:
            x2T_gather(3)
        sc.barrier()


def _core_consts(r):
    k = np.arange(128)[:, None, None]
    jj = np.arange(16)[None, :, None]
    q = np.arange(512)[None, None, :]
    mask = (((jj * 128 + k) <= (r * 512 + q)).astype(np.float32) * 3e38).astype(ml_dtypes.bfloat16).reshape(128, 16 * 512)
    halosel = np.zeros((4, 16), np.float32)
    prefsel = np.zeros((4, 16), np.float32)
    invc = np.zeros((4, 4, 16), np.float32)
    for m in range(4):
        G = 4 * m + r
        if G >= 1:
            halosel[m, G - 1] = 1.0
        prefsel[m, :G] = 1.0
        for g, w in enumerate(WINS):
            pos = G * 512 + np.arange(16) + 1
            invc[m, g] = 1.0 / np.minimum(pos, w)
    return {
        "c_mask": mask,
        "c_halosel": np.ascontiguousarray(np.broadcast_to(halosel.reshape(1, 64), (128, 64))),
        "c_prefsel": np.ascontiguousarray(np.broadcast_to(prefsel.reshape(1, 64), (8, 64))),
        "c_invc": np.ascontiguousarray(np.broadcast_to(invc.reshape(1, 256), (128, 256))),
        "c_identf": np.eye(128, dtype=np.float32),
        "c_swap": np.roll(np.eye(128, dtype=np.float32), 64, axis=0),
    }


def _own_rows(r):
    return np.concatenate([np.arange((4 * m + r) * 512, (4 * m + r + 1) * 512) for m in range(4)])


_PROG = {}
FUSED = True


def kernel(x, w_in, b_f, pool_w, pool_scale, w_out, ln1_g, ln1_b, w_mlp1, w_mlp2, ln2_g, ln2_b):
    f32 = lambda a: np.ascontiguousarray(np.asarray(a, dtype=np.float32))
    x = f32(x)
    ws = dict(w_in=f32(w_in), b_f=f32(b_f), pool_w=f32(pool_w), pool_scale=f32(pool_scale), w_out=f32(w_out),
              ln1_g=f32(ln1_g), ln1_b=f32(ln1_b), w_mlp1=f32(w_mlp1), w_mlp2=f32(w_mlp2), ln2_g=f32(ln2_g), ln2_b=f32(ln2_b))
    consts = [_core_consts(c % 4) for c in range(NCORES)]
    rows = [_own_rows(c % 4) for c in range(NCORES)]
    if FUSED:
        if "fused" not in _PROG:
            _PROG["fused"] = build_program(DEPTH, True)
        nc = _PROG["fused"]
        in_maps = []
        for c in range(NCORES):
            b = c // 4
            m = {"xTfull": np.ascontiguousarray(x[b].T), "xown": np.ascontiguousarray(x[b][rows[c]]),
                 "xownT": np.ascontiguousarray(x[b][rows[c]].T)}
            m.update(ws)
            m.update(consts[c])
            in_maps.append(m)
        res = run_bass_kernel_spmd(nc, in_maps, core_ids=list(range(NCORES)))
        out = np.empty_like(x)
        for c in range(NCORES):
            out[c // 4][rows[c]] = res.results[c]["y"]
        return out
    if "unfused" not in _PROG:
        _PROG["unfused"] = build_program(1, False)
    nc = _PROG["unfused"]
    cur = x
    for l in range(DEPTH):
        in_maps = []
        for c in range(NCORES):
            b = c // 4
            m = {"xfull": cur[b], "xown": np.ascontiguousarray(cur[b][rows[c]])}
            for k, v in ws.items():
                m[k] = np.ascontiguousarray(v[l:l + 1])
            m.update(consts[c])
            in_maps.append(m)
        res = run_bass_kernel_spmd(nc, in_maps, core_ids=list(range(NCORES)))
        nxt = np.empty_like(cur)
        for c in range(NCORES):
            nxt[c // 4][rows[c]] = res.results[c]["y"]
        cur = nxt
    return cur
```

```python
import contextlib
import numpy as np
import ml_dtypes
import concourse.bass as bass
import concourse.mybir as mybir
from concourse.bass_utils import run_bass_kernel_spmd

F32 = mybir.dt.float32
BF16 = mybir.dt.bfloat16
ACT = mybir.ActivationFunctionType
ALU = mybir.AluOpType
AX = mybir.AxisListType

D = 1024
S = 8192
B = 2
DEPTH = 2
NH = 8
HD = 64
DFF = 4096
INW = 2056
T = 2048
NG = 16
NB = 64
LN_EPS = 1e-5
ALPHA = float((2.0 * DEPTH) ** 0.25)
WINS = (2, 4, 8, 16)
NCORES = 8


class Sched:
    ENGS = ("pe", "act", "dve", "pool", "sp")

    def __init__(self, nc, stack):
        self.nc = nc
        self.stack = stack
        self.ops = {e: [] for e in self.ENGS}
        self.cnt = {e: 0 for e in self.ENGS}
        self.known = {e: {} for e in self.ENGS}
        self.lastw = {}
        self.readers = {}
        self.sem = {e: stack.enter_context(nc.semaphore("prog_" + e)) for e in ("pe", "act", "dve", "pool")}
        self.dsem = {}
        self.dcnt = {}

    def _dsem(self, name):
        if name not in self.dsem:
            self.dsem[name] = self.stack.enter_context(self.nc.semaphore("d_" + name))
            self.dcnt[name] = 0
        return self.dsem[name]

    def _deps(self, e, reads, writes):
        toks = []
        for k in reads:
            if k in self.lastw:
                toks.append(self.lastw[k])
        for k in writes:
            if k in self.lastw:
                toks.append(self.lastw[k])
            toks.extend(self.readers.get(k, ()))
        waits = []
        kn = self.known[e]
        for t in toks:
            if t[0] == "eng":
                _, e2, n2 = t
                if e2 == e and e == "pe":
                    continue
                key = ("eng", e2)
            else:
                _, e2, n2 = t
                key = ("dma", e2)
            if kn.get(key, 0) < n2:
                kn[key] = n2
                waits.append((key, n2))
        best = {}
        for key, n2 in waits:
            best[key] = max(best.get(key, 0), n2)
        out = []
        for key, n2 in best.items():
            s = self.sem[key[1]] if key[0] == "eng" else self.dsem[key[1]]
            out.append((s, n2))
        return out

    def op(self, e, fn, reads=(), writes=()):
        waits = self._deps(e, reads, writes)
        self.cnt[e] += 1
        n = self.cnt[e]
        self.ops[e].append((waits, fn, self.sem[e]))
        tok = ("eng", e, n)
        for k in reads:
            self.readers.setdefault(k, []).append(tok)
        for k in writes:
            self.lastw[k] = tok
            self.readers[k] = []

    def dma(self, q, semname, out, in_, reads=(), writes=(), slow=False):
        s = self._dsem(semname)
        waits = self._deps(q, reads, writes)
        self.dcnt[semname] += 16
        c = self.dcnt[semname]

        def fn(eng, out=out, in_=in_, s=s, slow=slow):
            if slow:
                eng.dma_start(out=out, in_=in_, allow_slow_non_contiguous=True).then_inc(s, 16)
            else:
                eng.dma_start(out=out, in_=in_).then_inc(s, 16)
            return None
        self.ops[q].append((waits, fn, None))
        tok = ("dma", semname, c)
        for k in reads:
            self.readers.setdefault(k, []).append(tok)
        for k in writes:
            self.lastw[k] = tok
            self.readers[k] = []

    def collective(self, semname, fn, reads=(), writes=()):
        s = self._dsem(semname)
        waits = self._deps("pool", reads, writes)
        self.dcnt[semname] += 1
        c = self.dcnt[semname]

        def f2(eng, fn=fn, s=s):
            fn(eng).then_inc(s, 1)
            return None
        self.ops["pool"].append((waits, f2, None))
        tok = ("dma", semname, c)
        for k in reads:
            self.readers.setdefault(k, []).append(tok)
        for k in writes:
            self.lastw[k] = tok
            self.readers[k] = []

    def barrier(self, exclude=("cc",)):
        for e in self.ENGS:
            kn = self.known[e]
            waits = []
            for e2 in ("pe", "act", "dve", "pool"):
                n2 = self.cnt[e2]
                if n2 > 0 and kn.get(("eng", e2), 0) < n2:
                    kn[("eng", e2)] = n2
                    waits.append((self.sem[e2], n2))
            for nm, c in self.dcnt.items():
                if nm in exclude:
                    continue
                if c > 0 and kn.get(("dma", nm), 0) < c:
                    kn[("dma", nm)] = c
                    waits.append((self.dsem[nm], c))
            if waits:
                self.ops[e].append((waits, None, None))

    def final_wait(self, q, semnames):
        for nm in semnames:
            s, c = self.dsem[nm], self.dcnt[nm]
            self.ops[q].append(([(s, c)], None, None))

    def emit(self, e, eng):
        for waits, fn, inc in self.ops[e]:
            for s, v in waits:
                eng.wait_ge(s, v)
            if fn is None:
                continue
            r = fn(eng)
            if inc is not None:
                r.then_inc(inc, 1)


def build_program(n_layers, fused, dbg=False):
    nc = bass.Bass("TRN2", target_bir_lowering=False)
    L = n_layers
    dbgt = {}
    if dbg:
        for nm, shp, dt in (("d_negc", [128, 512], F32), ("d_QT", [65, NH * T], BF16), ("d_mixT", [128, 8 * T], BF16),
                            ("d_x1", [128, 16 * D], F32), ("d_XT", [D, S], BF16), ("d_KT", [65, 2 * S], BF16),
                            ("d_V", [128, NB * 192], BF16), ("d_lfo", [8, T], F32), ("d_uT", [128, 4 * 528], F32),
                            ("d_diffT", [128, 4 * 512], BF16)):
            dbgt[nm] = nc.dram_tensor(nm, shp, dt, kind="ExternalOutput")

    def din(name, shape, dt=F32):
        return nc.dram_tensor(name, list(shape), dt, kind="ExternalInput")

    xfull = din("xfull", [S, D])
    xown = din("xown", [T, D])
    w_in = din("w_in", [L, D, INW])
    b_f = din("b_f", [L, NH])
    pool_w = din("pool_w", [L, 4, 128, 128])
    pool_scale = din("pool_scale", [L, 512])
    w_out = din("w_out", [L, D, D])
    ln1_g = din("ln1_g", [L, D])
    ln1_b = din("ln1_b", [L, D])
    w1 = din("w_mlp1", [L, D, DFF])
    w2 = din("w_mlp2", [L, DFF, D])
    ln2_g = din("ln2_g", [L, D])
    ln2_b = din("ln2_b", [L, D])
    c_mask = din("c_mask", [128, 16 * 512], BF16)
    c_halosel = din("c_halosel", [128, 64])
    c_prefsel = din("c_prefsel", [8, 64])
    c_invc = din("c_invc", [128, 256])
    c_identf = din("c_identf", [128, 128])
    c_swap = din("c_swap", [128, 128])
    y = nc.dram_tensor("y", [T, D], F32, kind="ExternalOutput")
    XT = nc.dram_tensor("XT", [NG, D, 512], BF16)
    ncs = nc.dram_tensor("ncs", [NH, 3, S], BF16)
    W1b = nc.dram_tensor("W1b", [8, 128, 8 * 512], BF16)
    W2b = nc.dram_tensor("W2b", [8, 128, 8 * 512], BF16)
    xown1 = nc.dram_tensor("xown1", [T, D], F32)
    agin = [nc.dram_tensor(f"agin{m}", [D, 512], BF16) for m in range(4)]
    agout = [nc.dram_tensor(f"agout{m}", [4 * D, 512], BF16) for m in range(4)]

    with contextlib.ExitStack() as st:
        sc = Sched(nc, st)

        def sb(name, shape, dt):
            return st.enter_context(nc.sbuf_tensor(name, list(shape), dt))

        ps = st.enter_context(nc.psum_tensor("ps", [128, 8, 512], F32))

        mixT = sb("mixT", [128, 8, T], BF16)
        identf = sb("identf", [128, 128], F32)
        swapm = sb("swapm", [128, 128], F32)
        halosel = sb("halosel", [128, 64], F32)
        prefsel = sb("prefsel", [8, 64], F32)
        invc = sb("invc", [128, 256], F32)
        negc = None
        pscale = sb("pscale", [128, 4], F32)
        bft = sb("bft", [8, 1], F32)
        negbf = sb("negbf", [8, 1], F32)
        small = sb("small", [128, 64], F32)

        sc.dma("sp", "c_identf", identf[:], c_identf[:, :], writes=["identf"])
        sc.dma("sp", "c_swapm", swapm[:], c_swap[:, :], writes=["swapm"])
        sc.dma("sp", "c_halosel", halosel[:], c_halosel[:, :], writes=["halosel"])
        sc.dma("sp", "c_prefsel", prefsel[:], c_prefsel[:, :], writes=["prefsel"])
        sc.dma("sp", "c_invc", invc[:], c_invc[:, :], writes=["invc"])

        for l in range(L):
            layer(nc, sc, st, l, locals())

        sc.final_wait("sp", ["yout"] + ["dbg_" + k for k in dbgt])

        with nc.Block() as block:
            @block.tensor
            def _(e):
                sc.emit("pe", e)

            @block.scalar
            def _(e):
                sc.emit("act", e)

            @block.vector
            def _(e):
                sc.emit("dve", e)

            @block.gpsimd
            def _(e):
                sc.emit("pool", e)

            @block.sync
            def _(e):
                sc.emit("sp", e)
    return nc


def layer(nc, sc, st_outer, l, env):
    g = env
    ps = g["ps"]
    mixT, identf, swapm = g["mixT"], g["identf"], g["swapm"]
    halosel, prefsel, invc, negc = g["halosel"], g["prefsel"], g["invc"], g["negc"]
    pscale, bft, negbf, small = g["pscale"], g["bft"], g["negbf"], g["small"]
    xfull, xown, XT, y = g["xfull"], g["xown"], g["XT"], g["y"]
    w_in, b_f, pool_w, pool_scale, w_out = g["w_in"], g["b_f"], g["pool_w"], g["pool_scale"], g["w_out"]
    ln1_g, ln1_b, w1, w2, ln2_g, ln2_b = g["ln1_g"], g["ln1_b"], g["w1"], g["w2"], g["ln2_g"], g["ln2_b"]
    c_mask = g["c_mask"]
    dbgt = g["dbgt"]
    L = f"L{l}_"
    nlayers = g["L"]
    last = (l == nlayers - 1)
    if l > 0:
        xown = g["xown1"]
    agin, agout, xown1 = g["agin"], g["agout"], g["xown1"]

    def dump(name, src, reads):
        if name in dbgt:
            sc.dma("sp", "dbg_" + name, dbgt[name].ap(), src, reads=reads, writes=["dbgout_" + name])

    def PS(b, n=1):
        if n == 1:
            return ps[:, b, :]
        return ps[:, b:b + n, :]

    def psk(b):
        return f"ps{b}"

    win_v = w_in[l].rearrange("(kc p) c -> p kc c", p=128)

    sc.dma("sp", "c_pscale", pscale[:], pool_scale[l].rearrange("(g p) -> p g", p=128), writes=["pscale"], slow=True)
    sc.dma("sp", "c_bft", bft[:], b_f[l].rearrange("(h o) -> h o", o=1), writes=["bft"])
    sc.op("dve", lambda e: e.tensor_scalar(out=negbf[:], in0=bft[:], scalar1=-1.0, scalar2=None, op0=ALU.mult),
          reads=["bft"], writes=["negbf"])

    XTv = XT.ap().rearrange("c (p kc) t -> c p kc t", kc=8)
    agov = [a.ap().rearrange("(r p kc) t -> r p kc t", r=4, kc=8) for a in agout]
    aginv = [a.ap().rearrange("(p kc) t -> p kc t", kc=8) for a in agin]
    W1b, W2b = g["W1b"], g["W2b"]
    ncs = g["ncs"]

    def xt_chunk(c):
        if l == 0:
            return XTv[c]
        return agov[c // 4][c % 4]

    with contextlib.ExitStack() as s1:
        def sb(name, shape, dt):
            return s1.enter_context(nc.sbuf_tensor(L + name, list(shape), dt))

        QT = sb("QT", [68, NH, T], BF16)
        xtail = sb("xtail", [128, 8, NG * 16], BF16)
        gs = sb("gs", [8, NG], F32)
        xTc = [sb(f"xTc{i}", [128, 8, 512], BF16) for i in range(2)]
        Wf = sb("Wf", [128, 8, 8], BF16)
        sc.dma("pool", "wf", Wf[:], win_v[:, :, 2048:2056], writes=["Wf"])
        sF = contextlib.ExitStack()
        lfT = sF.enter_context(nc.sbuf_tensor(L + "lfT", [8, S], F32))
        cumT = sF.enter_context(nc.sbuf_tensor(L + "cumT", [8, S], F32))
        zer = sF.enter_context(nc.sbuf_tensor(L + "zer", [8, 512], F32))
        spl = sF.enter_context(nc.sbuf_tensor(L + "spl", [8, 3, 512], BF16))
        stmp = sF.enter_context(nc.sbuf_tensor(L + "stmp", [8, 512], F32))
        sc.op("pool", lambda e: e.memset(zer[:], 0.0), writes=["zer"])

        def f_tail(c):
            sl = slice(c * 512, (c + 1) * 512)
            sc.op("act", lambda e, o=lfT[:, sl]: e.activation(out=o, in_=o, func=ACT.Ln, bias=1.0, scale=1.0),
                  reads=[f"lfT{c}"], writes=[f"lfT{c}"])
            sc.op("dve", lambda e, o=gs[:, c:c + 1], i=lfT[:, sl]: e.tensor_reduce(out=o, in_=i, axis=AX.X, op=ALU.add),
                  reads=[f"lfT{c}"], writes=["gs"])
            init = 0.0 if c == 0 else cumT[:, c * 512 - 1:c * 512]
            sc.op("dve", lambda e, o=cumT[:, sl], d0=lfT[:, sl], ini=init:
                  e.tensor_tensor_scan(out=o, data0=d0, data1=zer[:], initial=ini, op0=ALU.add, op1=ALU.add),
                  reads=[f"lfT{c}", "zer", "cumT"], writes=["cumT"])
            sc.op("dve", lambda e, i=cumT[:, sl]: e.tensor_copy(out=spl[:, 0, :], in_=i), reads=["cumT", "spl"], writes=["spl"])
            sc.op("dve", lambda e, i=cumT[:, sl]: e.tensor_tensor(out=stmp[:], in0=i, in1=spl[:, 0, :], op=ALU.subtract),
                  reads=["cumT", "spl"], writes=["stmp"])
            sc.op("dve", lambda e: e.tensor_copy(out=spl[:, 1, :], in_=stmp[:]), reads=["stmp"], writes=["spl"])
            sc.op("dve", lambda e: e.tensor_tensor(out=stmp[:], in0=stmp[:], in1=spl[:, 1, :], op=ALU.subtract),
                  reads=["stmp", "spl"], writes=["stmp"])
            sc.op("dve", lambda e: e.tensor_copy(out=spl[:, 2, :], in_=stmp[:]), reads=["stmp"], writes=["spl"])
            sc.dma("sp", "ncs", ncs.ap()[:, :, sl], spl[:], reads=["spl"], writes=["ncs"])

        if l == 0:
            with contextlib.ExitStack() as sA:
                xin = [sA.enter_context(nc.sbuf_tensor(L + f"xin{i}", [128, 4, D], F32)) for i in range(2)]
                xts = [sA.enter_context(nc.sbuf_tensor(L + f"xts{i}", [128, 8, 512], BF16)) for i in range(2)]
                def loadA(c):
                    sc.dma("sp", f"xin{c % 2}", xin[c % 2][:], xfull[c * 512:(c + 1) * 512, :].rearrange("(b p) d -> p b d", p=128),
                           writes=[f"xin{c % 2}"])
                loadA(0)
                for c in range(NG):
                    s_ = c % 2
                    if c + 1 < NG:
                        loadA(c + 1)
                    for blk in range(4):
                        pb = (blk % 2) * 2
                        for kc in range(8):
                            sc.op("pe", lambda e, o=ps[:, pb + kc // 4, (kc % 4) * 128:(kc % 4 + 1) * 128],
                                  i=xin[s_][:, blk, kc * 128:(kc + 1) * 128]: e.transpose(out=o, in_=i, identity=identf[:]),
                                  reads=[f"xin{s_}", "identf"], writes=[psk(pb + kc // 4)])
                        src = ps[:, pb:pb + 2, :].rearrange("p a (k t) -> p (a k) t", t=128)
                        dst = xts[s_][:, :, blk * 128:(blk + 1) * 128]
                        if blk % 2 == 0:
                            sc.op("act", lambda e, o=dst, i=src: e.activation(out=o, in_=i, func=ACT.Copy),
                                  reads=[psk(pb), psk(pb + 1)], writes=[f"xts{s_}"])
                        else:
                            sc.op("dve", lambda e, o=dst, i=src: e.tensor_copy(out=o, in_=i),
                                  reads=[psk(pb), psk(pb + 1)], writes=[f"xts{s_}"])
                    sc.op("pool", lambda e, o=xtail[:, :, c * 16:(c + 1) * 16], i=xts[s_][:, :, 496:512]: e.tensor_copy(out=o, in_=i),
                          reads=[f"xts{s_}"], writes=["xtail"])
                    sc.dma("sp", f"xts{s_}", XTv[c], xts[s_][:], reads=[f"xts{s_}"], writes=["XT"])
                    pbf = 4 + c % 2
                    for kc in range(8):
                        sc.op("pe", lambda e, o=ps[0:8, pbf, :], a=Wf[:, kc, :], b=xts[s_][:, kc, :], k=kc:
                              e.matmul(o, lhsT=a, rhs=b, start=(k == 0), stop=(k == 7)),
                              reads=["Wf", f"xts{s_}"], writes=[psk(pbf)])
                    sc.op("act", lambda e, o=lfT[:, c * 512:(c + 1) * 512], i=ps[0:8, pbf, :]:
                          e.activation(out=o, in_=i, func=ACT.Exp, bias=negbf[:], scale=-1.0),
                          reads=[psk(pbf), "negbf"], writes=[f"lfT{c}"])
                    f_tail(c)


            sc.barrier()
        with contextlib.ExitStack() as sA:
            xin = [sA.enter_context(nc.sbuf_tensor(L + f"pxin{i}", [128, 4, D], F32)) for i in range(1)] * 2
            xts = [sA.enter_context(nc.sbuf_tensor(L + f"pxts{i}", [128, 8, 512], BF16)) for i in range(2)]
            Wq = sA.enter_context(nc.sbuf_tensor(L + "Wq", [128, 8, 512], BF16))
            sc.dma("pool", "wq", Wq[:], win_v[:, :, 512:1024], writes=["Wq"])

            def loadX(m):
                sc.dma("sp", "pxin", xin[0][:], xown[m * 512:(m + 1) * 512, :].rearrange("(b p) d -> p b d", p=128),
                       reads=["xown1"], writes=["pxin"])
            if l == 0:
                loadX(0)
            for m in range(4):
                s_ = m % 2
                if l > 0:
                    if m == 0:
                        sc.dma("sp", f"xts{s_}", xts[s_][:], aginv[m], reads=[f"agin{m}"], writes=[f"xts{s_}"])
                    if m + 1 < 4:
                        sc.dma("sp", f"xts{(m + 1) % 2}", xts[(m + 1) % 2][:], aginv[m + 1], reads=[f"agin{m + 1}"], writes=[f"xts{(m + 1) % 2}"])
                for blk in (range(4) if l == 0 else ()):
                    pb = (blk % 2) * 2
                    for kc in range(8):
                        sc.op("pe", lambda e, o=ps[:, pb + kc // 4, (kc % 4) * 128:(kc % 4 + 1) * 128],
                              i=xin[0][:, blk, kc * 128:(kc + 1) * 128]: e.transpose(out=o, in_=i, identity=identf[:]),
                              reads=["pxin", "identf"], writes=[psk(pb + kc // 4)])
                    src = ps[:, pb:pb + 2, :].rearrange("p a (k t) -> p (a k) t", t=128)
                    dst = xts[s_][:, :, blk * 128:(blk + 1) * 128]
                    if blk % 2 == 0:
                        sc.op("act", lambda e, o=dst, i=src: e.activation(out=o, in_=i, func=ACT.Copy),
                              reads=[psk(pb), psk(pb + 1)], writes=[f"xts{s_}"])
                    else:
                        sc.op("dve", lambda e, o=dst, i=src: e.tensor_copy(out=o, in_=i),
                              reads=[psk(pb), psk(pb + 1)], writes=[f"xts{s_}"])
                if l == 0 and m + 1 < 4:
                    loadX(m + 1)
                if l == 0:
                    sc.dma("sp", f"xts{s_}", aginv[m], xts[s_][:], reads=[f"xts{s_}"], writes=[f"agin{m}"])
                if False:
                    sc.collective("cc", lambda e, m=m: e.collective_compute("AllGather", ALU.bypass, replica_groups=[[0, 1, 2, 3], [4, 5, 6, 7]],
                                                                            ins=[agin[m].ap().opt()], outs=[agout[m].ap().opt()]),
                                  reads=[f"agin{m}"], writes=["XT"])
                for h in range(NH):
                    pb = 4 + h % 2
                    for kc in range(8):
                        sc.op("pe", lambda e, o=ps[0:64, pb, :], a=Wq[:, kc, h * 64:(h + 1) * 64], b=xts[s_][:, kc, :], k=kc:
                              e.matmul(o, lhsT=a, rhs=b, start=(k == 0), stop=(k == 7)),
                              reads=["Wq", f"xts{s_}"], writes=[psk(pb)])
                    sc.op("act", lambda e, o=QT[0:64, h, m * 512:(m + 1) * 512], i=ps[0:64, pb, :]:
                          e.activation(out=o, in_=i, func=ACT.Copy, scale=0.125),
                          reads=[psk(pb)], writes=[f"QT{h}"])
        sc.barrier()
        with contextlib.ExitStack() as sC:
            def loadC(c):
                sc.dma("sp", f"xTc{c % 2}", xTc[c % 2][:], xt_chunk(c), reads=["XT"], writes=[f"xTc{c % 2}"])
            if l > 0:
                loadC(0)
            for c in (range(NG) if l > 0 else ()):
                s_ = c % 2
                if c + 1 < NG:
                    loadC(c + 1)
                pb = c % 2
                if l > 0:
                    sc.op("pool", lambda e, o=xtail[:, :, c * 16:(c + 1) * 16], i=xTc[s_][:, :, 496:512]: e.tensor_copy(out=o, in_=i),
                          reads=[f"xTc{s_}"], writes=["xtail"])
                for kc in range(8):
                    sc.op("pe", lambda e, o=ps[0:8, pb, :], a=Wf[:, kc, :], b=xTc[s_][:, kc, :], k=kc:
                          e.matmul(o, lhsT=a, rhs=b, start=(k == 0), stop=(k == 7)),
                          reads=["Wf", f"xTc{s_}"], writes=[psk(pb)])
                sc.op("act", lambda e, o=lfT[:, c * 512:(c + 1) * 512], i=ps[0:8, pb, :]:
                      e.activation(out=o, in_=i, func=ACT.Exp, bias=negbf[:], scale=-1.0),
                      reads=[psk(pb), "negbf"], writes=[f"lfT{c}"])
                f_tail(c)
            pass

        sc.barrier()
        sF.close()
        mask = sb("mask", [128, 16, 512], BF16)
        Wk = [sb(f"Wk{i}", [128, 8, 128], BF16) for i in range(2)]
        Wv = [sb(f"Wv{i}", [128, 8, 128], BF16) for i in range(2)]
        sc.dma("sp", "c_mask", mask[:], c_mask[:, :].rearrange("p (j q) -> p j q", q=512), writes=["mask"])
        for w_ in range(1):
            sc.dma("pool", f"wk{w_}", Wk[w_][:], win_v[:, :, 1024:1024 + 128], writes=[f"Wk{w_}"])
            sc.dma("pool", f"wv{w_}", Wv[w_][:], win_v[:, :, 1536:1536 + 128], writes=[f"Wv{w_}"])
        sc.dma("sp", "xTc0", xTc[0][:], xt_chunk(0), reads=["XT"], writes=["xTc0"])
        Wu = sb("Wu", [128, 8, 512], BF16)
        pw = sb("pw", [128, 4, 128], BF16)
        sc.dma("pool", "wu", Wu[:], win_v[:, :, 0:512], writes=["Wu"])
        sc.dma("pool", "pw", pw[:], pool_w[l].rearrange("g c d -> c g d"), writes=["pw"])
        with contextlib.ExitStack() as sB:
            def sbB(name, shape, dt):
                return sB.enter_context(nc.sbuf_tensor(L + name, list(shape), dt))
            xTo = [sbB(f"xTo{i}", [128, 8, 512], BF16) for i in range(2)]
            uT = sbB("uT", [128, 4, 528], F32)
            s2 = sbB("s2", [128, 4, 528], F32)
            s4 = sbB("s4", [128, 4, 528], F32)
            utail = sbB("utail", [128, 4, NG * 16], F32)
            htmp = sbB("htmp", [128, 4, 16, 16], F32)
            diffT = sbB("diffT", [128, 4, 512], BF16)
            dtmp = sbB("dtmp", [128, 4, 16], F32)
            lfo = sbB("lfo", [8, T], F32)
            cumo = sbB("cumo", [8, T], F32)
            aq = sbB("aq", [8, T], BF16)
            pref = sbB("pref", [8, 4], F32)
            ptmp = sbB("ptmp", [8, NG], F32)
            zerB = sbB("zerB", [8, 512], F32)
            sc.op("pool", lambda e: e.memset(zerB[:], 0.0), writes=["zerB"])

            for gq in range(4):
                for kc in range(8):
                    sc.op("pe", lambda e, o=ps[:, 3, 0:256], a=Wu[:, kc, gq * 128:(gq + 1) * 128], b=xtail[:, kc, :], k=kc:
                          e.matmul(o, lhsT=a, rhs=b, start=(k == 0), stop=(k == 7)),
                          reads=["Wu", "xtail"], writes=[psk(3)])
                sc.op("dve", lambda e, o=utail[:, gq, :], i=ps[:, 3, 0:256]: e.tensor_copy(out=o, in_=i),
                      reads=[psk(3)], writes=["utail"])

            for m in range(4):
                s_ = m % 2
                if m == 0:
                    sc.dma("sp", f"xTo{s_}", xTo[s_][:], aginv[m], reads=[f"agin{m}"], writes=[f"xTo{s_}"])
                if m + 1 < 4:
                    sc.dma("sp", f"xTo{(m + 1) % 2}", xTo[(m + 1) % 2][:], aginv[m + 1], reads=[f"agin{m + 1}"], writes=[f"xTo{(m + 1) % 2}"])
                for kc in range(8):
                    sc.op("pe", lambda e, o=ps[0:8, 6, :], a=Wf[:, kc, :], b=xTo[s_][:, kc, :], k=kc:
                          e.matmul(o, lhsT=a, rhs=b, start=(k == 0), stop=(k == 7)),
                          reads=["Wf", f"xTo{s_}"], writes=[psk(6)])
                sc.op("act", lambda e, o=lfo[:, m * 512:(m + 1) * 512], i=ps[0:8, 6, :]:
                      e.activation(out=o, in_=i, func=ACT.Exp, bias=negbf[:], scale=-1.0),
                      reads=[psk(6), "negbf"], writes=["lfo"])
                sc.op("act", lambda e, o=lfo[:, m * 512:(m + 1) * 512]:
                      e.activation(out=o, in_=o, func=ACT.Ln, bias=1.0, scale=1.0),
                      reads=["lfo"], writes=["lfo"])
                sc.op("dve", lambda e, i1=prefsel[:, m * 16:(m + 1) * 16]: e.tensor_tensor(out=ptmp[:], in0=gs[:], in1=i1, op=ALU.mult),
                      reads=["gs", "prefsel"], writes=["ptmp"])
                sc.op("dve", lambda e, o=pref[:, m:m + 1]: e.tensor_reduce(out=o, in_=ptmp[:], axis=AX.X, op=ALU.add),
                      reads=["ptmp"], writes=["pref"])
                sc.op("dve", lambda e, o=cumo[:, m * 512:(m + 1) * 512], d0=lfo[:, m * 512:(m + 1) * 512], ini=pref[:, m:m + 1]:
                      e.tensor_tensor_scan(out=o, data0=d0, data1=zerB[:], initial=ini, op0=ALU.add, op1=ALU.add),
                      reads=["lfo", "zerB", "pref"], writes=["cumo"])
                sc.op("dve", lambda e, o=aq[:, m * 512:(m + 1) * 512], i=cumo[:, m * 512:(m + 1) * 512]:
                      e.tensor_scalar(out=o, in0=i, scalar1=-1.0, scalar2=None, op0=ALU.mult),
                      reads=["cumo"], writes=["aq"])

                for gq in range(4):
                    pb = 4 + gq % 2
                    for kc in range(8):
                        sc.op("pe", lambda e, o=ps[:, pb, :], a=Wu[:, kc, gq * 128:(gq + 1) * 128], b=xTo[s_][:, kc, :], k=kc:
                              e.matmul(o, lhsT=a, rhs=b, start=(k == 0), stop=(k == 7)),
                              reads=["Wu", f"xTo{s_}"], writes=[psk(pb)])
                    sc.op("act", lambda e, o=uT[:, gq, 16:528], i=ps[:, pb, :]: e.activation(out=o, in_=i, func=ACT.Copy),
                          reads=[psk(pb)], writes=["uT"])
                sc.op("dve", lambda e, sel=halosel[:, m * 16:(m + 1) * 16].unsqueeze(1).unsqueeze(1).to_broadcast([128, 4, 16, 16]):
                      e.tensor_tensor(out=htmp[:], in0=utail[:].rearrange("p g (G t) -> p g t G", t=16), in1=sel, op=ALU.mult),
                      reads=["utail", "halosel"], writes=["htmp"])
                sc.op("dve", lambda e: e.tensor_reduce(out=uT[:, :, 0:16], in_=htmp[:], axis=AX.X, op=ALU.add),
                      reads=["htmp", "uT"], writes=["uT"])
                sc.op("dve", lambda e: e.tensor_tensor(out=s2[:, :, 1:528], in0=uT[:, :, 1:528], in1=uT[:, :, 0:527], op=ALU.add),
                      reads=["uT"], writes=["s2"])
                sc.op("dve", lambda e: e.tensor_tensor(out=s4[:, 1:4, 3:528], in0=s2[:, 1:4, 3:528], in1=s2[:, 1:4, 1:526], op=ALU.add),
                      reads=["s2"], writes=["s4"])
                def diff(gq, ssum, w, m=m):
                    sc.op("dve", lambda e: e.scalar_tensor_tensor(out=diffT[:, gq, 16:512], in0=ssum[:, gq, 32:528], scalar=1.0 / w,
                                                                  in1=uT[:, gq, 32:528], op0=ALU.mult, op1=ALU.subtract),
                          reads=["s2", "s4", "uT"], writes=["diffT"])
                    sc.op("dve", lambda e: e.tensor_tensor(out=dtmp[:, gq, :], in0=ssum[:, gq, 16:32],
                                                           in1=invc[:, (m * 4 + gq) * 16:(m * 4 + gq + 1) * 16], op=ALU.mult),
                          reads=["s2", "s4", "invc"], writes=["dtmp"])
                    sc.op("dve", lambda e: e.tensor_tensor(out=diffT[:, gq, 0:16], in0=dtmp[:, gq, :], in1=uT[:, gq, 16:32], op=ALU.subtract),
                          reads=["dtmp", "uT"], writes=["diffT"])
                diff(0, s2, 2)
                diff(1, s4, 4)
                sc.op("dve", lambda e: e.tensor_tensor(out=s2[:, 2:4, 7:528], in0=s4[:, 2:4, 7:528], in1=s4[:, 2:4, 3:524], op=ALU.add),
                      reads=["s4", "diffT", "dtmp"], writes=["s2"])
                diff(2, s2, 8)
                sc.op("dve", lambda e: e.tensor_tensor(out=s4[:, 3:4, 15:528], in0=s2[:, 3:4, 15:528], in1=s2[:, 3:4, 7:520], op=ALU.add),
                      reads=["s2", "diffT", "dtmp"], writes=["s4"])
                diff(3, s4, 16)
                for gq in range(4):
                    pb = 4 + gq % 2
                    sc.op("pe", lambda e, o=ps[:, pb, :], a=pw[:, gq, :], b=diffT[:, gq, :]: e.matmul(o, lhsT=a, rhs=b, start=True, stop=True),
                          reads=["pw", "diffT"], writes=[psk(pb)])
                    sc.op("act", lambda e, o=mixT[:, gq, m * 512:(m + 1) * 512], i=ps[:, pb, :], s=pscale[:, gq:gq + 1]:
                          e.activation(out=o, in_=i, func=ACT.Copy, scale=s),
                          reads=[psk(pb), "pscale"], writes=[f"mixT{gq}"])
            sc.op("pool", lambda e: e.memset(QT[64:68, :, :], 1.0), writes=["QTaug"])
            for h in range(NH):
                sc.dma("sp", "aqmv", QT[64:65, h, :], aq[h:h + 1, :], reads=["aq", "QTaug"], writes=["QTaug"])
            dump("d_QT", QT[:].rearrange("p h t -> p (h t)"), [f"QT{h}" for h in range(NH)] + ["QTaug"])
            dump("d_lfo", cumo[:], ["cumo"])
            dump("d_uT", uT[:].rearrange("p g t -> p (g t)"), ["uT"])
            dump("d_diffT", diffT[:].rearrange("p g t -> p (g t)"), ["diffT"])

        sc.barrier()
        with contextlib.ExitStack() as sT:
            def sbT(name, shape, dt):
                return sT.enter_context(nc.sbuf_tensor(L + name, list(shape), dt))
            KT = sbT("KT", [68, 2, S], BF16)
            Vaug = sbT("Vaug", [128, NB, 192], BF16)
            Pt = [sbT(f"Pt{i}", [128, 512], BF16) for i in range(3)]
            RA = sbT("RA", [128, 512], F32)
            RB = sbT("RB", [128, 512], F32)
            Wsb = sbT("Wsb", [128, 512], F32)
            ktmp = [sbT(f"ktmp{i}", [128, 512], BF16) for i in range(2)]
            sc.op("pool", lambda e: e.memset(KT[64:65, :, :], 1.0), writes=["KTaug"])
            sc.op("pool", lambda e: e.memset(Vaug[:, :, 64:128], 1.0), writes=["Vones"])

            def loadW(hp_):
                w_ = hp_ % 2
                sc.dma("pool", f"wk{w_}", Wk[w_][:], win_v[:, :, 1024 + hp_ * 128:1024 + (hp_ + 1) * 128], writes=[f"Wk{w_}"])
                sc.dma("pool", f"wv{w_}", Wv[w_][:], win_v[:, :, 1536 + hp_ * 128:1536 + (hp_ + 1) * 128], writes=[f"Wv{w_}"])

            xTcA = [xTc[0], xTc[1], sbT("xTc2", [128, 8, 512], BF16), sbT("xTc3", [128, 8, 512], BF16)]

            def loadK(c):
                sc.dma("sp", f"xTc{c % 4}", xTcA[c % 4][:], xt_chunk(c), reads=["XT"], writes=[f"xTc{c % 4}"])
            for c_ in (1, 2, 3):
                loadK(c_)
            w1v_ = w1[l].rearrange("(kc p) c -> p kc c", p=128)
            w2v_ = w2[l].rearrange("(f p) c -> p f c", p=128)
            casts = []
            for s_i in range(8):
                casts.append(("wcast1", W1b.ap()[s_i].rearrange("p (k c) -> p k c", c=512), w1v_[:, :, s_i * 512:(s_i + 1) * 512], "W1b"))
            for half in range(2):
                for sc_ in range(4):
                    casts.append(("wcast2", W2b.ap()[half * 4 + sc_].rearrange("p (k c) -> p k c", c=512),
                                  w2v_[:, sc_ * 8:(sc_ + 1) * 8, half * 512:(half + 1) * 512], "W2b"))
            pending = [None]
            for hp in range(4):
                ws = hp % 2
                for hh_ in range(2):
                    sc.dma("sp", f"ktn{hh_}", KT[65:68, hh_, :], ncs.ap()[2 * hp + hh_], reads=["ncs"], writes=[f"KTn{hh_}"])
                for c in range(NG):
                    s_ = c % 4
                    pb = 6 + (c % 2)
                    kt_ = c % 2
                    for kc in range(8):
                        sc.op("pe", lambda e, o=ps[:, pb, :], a=Wk[ws][:, kc, :], b=xTcA[s_][:, kc, :], k=kc:
                              e.matmul(o, lhsT=a, rhs=b, start=(k == 0), stop=(k == 7)),
                              reads=[f"Wk{ws}", f"xTc{s_}"], writes=[psk(pb)])
                    sc.op("act", lambda e, o=KT[0:64, 0, c * 512:(c + 1) * 512], i=ps[0:64, pb, :]:
                          e.activation(out=o, in_=i, func=ACT.Copy),
                          reads=[psk(pb)], writes=["KT0"])
                    sc.op("act", lambda e, o=ktmp[kt_][64:128, :], i=ps[64:128, pb, :]: e.activation(out=o, in_=i, func=ACT.Copy),
                          reads=[psk(pb)], writes=[f"ktmp{kt_}"])
                    sc.dma("sp", f"ktB{kt_}", KT[0:64, 1, c * 512:(c + 1) * 512], ktmp[kt_][64:128, :],
                           reads=[f"ktmp{kt_}"], writes=[f"KT1_{kt_}"])
                    pb = 5
                    for blk in range(4):
                        for kc in range(8):
                            sc.op("pe", lambda e, o=ps[:, pb, blk * 128:(blk + 1) * 128], a=xTcA[s_][:, kc, blk * 128:(blk + 1) * 128],
                                  b=Wv[ws][:, kc, :], k=kc: e.matmul(o, lhsT=a, rhs=b, start=(k == 0), stop=(k == 7)),
                                  reads=[f"Wv{ws}", f"xTc{s_}"], writes=[psk(pb)])
                    pv = ps[:, pb, :].rearrange("p (b c) -> p b c", c=128)
                    sc.op("dve", lambda e, o=Vaug[:, c * 4:(c + 1) * 4, 0:64], i=pv[:, :, 0:64]: e.tensor_copy(out=o, in_=i),
                          reads=[psk(pb)], writes=["VA"])
                    sc.op("dve", lambda e, o=Vaug[:, c * 4:(c + 1) * 4, 128:192], i=pv[:, :, 64:128]: e.tensor_copy(out=o, in_=i),
                          reads=[psk(pb)], writes=["VB"])
                    if c + 4 < NG:
                        loadK(c + 4)

                if hp + 1 < 4:
                    loadW(hp + 1)
                    for c_ in range(4):
                        loadK(c_)
                for sem_, dst_, src_, key_ in casts[hp * 4:(hp + 1) * 4]:
                    sc.dma("pool", sem_, dst_, src_, reads=["VB", "KT0"], writes=[key_])
                if hp == 0:
                    dump("d_KT", KT[:].rearrange("p h t -> p (h t)"), ["KT0", "KT1_0", "KT1_1", "KTaug"])
                    dump("d_V", Vaug[:].rearrange("p b c -> p (b c)"), ["VA", "VB", "Vones"])
                for hh in range(2):
                    h = 2 * hp + hh
                    vk = "VA" if hh == 0 else "VB"
                    for m in range(4):
                        NJ = 16 * m + 16
                        ob = 3 + (m % 2)
                        q_ap = QT[0:68, h, m * 512:(m + 1) * 512]

                        def qk(j):
                            sb_ = j % 3
                            sc.op("pe", lambda e, o=ps[:, sb_, :], a=KT[0:68, hh, j * 128:(j + 1) * 128], q_ap=q_ap:
                                  e.matmul(o, lhsT=a, rhs=q_ap, start=True, stop=True),
                                  reads=(["KT0"] if hh == 0 else ["KT1_0", "KT1_1"]) + ["KTaug", f"KTn{hh}", f"QT{h}", "QTaug"], writes=[psk(sb_)])
                        qk(0)
                        qk(1)
                        for j in range(NJ):
                            if j == 6 and pending[0] is not None:
                                pending[0]()
                                pending[0] = None
                            if j + 2 < NJ:
                                qk(j + 2)
                            sb_ = j % 3
                            sc.op("act", lambda e, o=Pt[sb_][:], i=ps[:, sb_, :]:
                                  e.activation(out=o, in_=i, func=ACT.Exp),
                                  reads=[psk(sb_)], writes=[f"Pt{sb_}"])
                            if j >= 16 * m:
                                sc.op("dve", lambda e, o=Pt[sb_][:], mk=mask[:, j - 16 * m, :]:
                                      e.tensor_tensor(out=o, in0=o, in1=mk, op=ALU.min),
                                      reads=[f"Pt{sb_}", "mask"], writes=[f"Pt{sb_}"])
                            sc.op("pe", lambda e, o=ps[:, ob, :], a=Vaug[:, j, hh * 64:hh * 64 + 128], b=Pt[sb_][:], jj=j, NJ=NJ:
                                  e.matmul(o, lhsT=a, rhs=b, start=(jj == 0), stop=(jj == NJ - 1)),
                                  reads=[vk, "Vones", f"Pt{sb_}"], writes=[psk(ob)])
                        if hh == 0:
                            R, sums, outs = RA, slice(64, 128), slice(0, 64)
                        else:
                            R, sums, outs = RB, slice(0, 64), slice(64, 128)
                        rk = "RA" if hh == 0 else "RB"
                        sc.op("dve", lambda e, o=R[sums, :], i=ps[sums, ob, :]: e.reciprocal(out=o, in_=i),
                              reads=[psk(ob), rk], writes=[rk])
                        sc.dma("sp", "nrm", Wsb[outs, :], R[sums, :], reads=[rk], writes=["Wsb"])

                        def fin(o=mixT[outs, 4 + hp, m * 512:(m + 1) * 512], i0=ps[outs, ob, :], i1=Wsb[outs, :], ob=ob, key=f"mixT{4 + hp}_{hh}"):
                            sc.op("dve", lambda e: e.tensor_tensor(out=o, in0=i0, in1=i1, op=ALU.mult),
                                  reads=[psk(ob), "Wsb"], writes=[key])
                        pending[0] = fin
            if pending[0] is not None:
                pending[0]()
                pending[0] = None

    sc.barrier()
    with contextlib.ExitStack() as s2_:
        def sb2(name, shape, dt):
            return s2_.enter_context(nc.sbuf_tensor(L + name, list(shape), dt))
        xres = sb2("xres", [128, 16, D], F32)
        zt = sb2("zt", [128, 4, D], F32)
        xn = sb2("xn", [128, D], F32)
        stt = sb2("stt", [128, 12], F32)
        gb = [None] * 4
        sD = contextlib.ExitStack()
        Wo = sD.enter_context(nc.sbuf_tensor(L + "Wo", [128, 8, D], BF16))
        gb[0] = sD.enter_context(nc.sbuf_tensor(L + "gb0", [128, D], F32))
        gb[1] = sD.enter_context(nc.sbuf_tensor(L + "gb1", [128, D], F32))

        sc.dma("sp", "xres", xres[:], xown.ap().rearrange("(lb p) d -> p lb d", p=128), reads=["xown1"], writes=["xres"])
        sc.dma("pool", "wo", Wo[:], w_out[l].rearrange("(mc p) c -> p mc c", p=128), writes=["Wo"])
        for i, v in enumerate((ln1_g, ln1_b)):
            sc.dma("sp", f"gb{i}", gb[i][:], v[l].partition_broadcast(128), writes=[f"gb{i}"])

        mixkeys = [f"mixT{i}" for i in range(4)] + [f"mixT{4 + hp}_{hh}" for hp in range(4) for hh in range(2)]

        def ln_steps(zsrc, zkey, gi, bi, lb, final):
            st_ = []
            st_.append(lambda: sc.op("dve", lambda e: e.bn_stats(out=stt[:, 0:6], in_=zsrc[:, 0:512]), reads=[zkey], writes=["stt"]))
            st_.append(lambda: sc.op("dve", lambda e: e.bn_stats(out=stt[:, 6:12], in_=zsrc[:, 512:1024]), reads=[zkey, "stt"], writes=["stt"]))
            st_.append(lambda: sc.op("dve", lambda e: e.bn_aggr(out=small[:, 0:2], in_=stt[:, :]), reads=["stt"], writes=["small"]))
            st_.append(lambda: sc.op("act", lambda e: e.activation(out=small[:, 2:3], in_=small[:, 1:2], func=ACT.Sqrt, bias=small[:, 8:9], scale=1.0),
                                     reads=["small", "epsc"], writes=["small_sd"]))
            st_.append(lambda: sc.op("dve", lambda e: e.reciprocal(out=small[:, 3:4], in_=small[:, 2:3]), reads=["small_sd"], writes=["small_r"]))
            st_.append(lambda: sc.op("dve", lambda e: e.tensor_scalar(out=small[:, 4:5], in0=small[:, 0:1], scalar1=small[:, 3:4], scalar2=-1.0,
                                                                      op0=ALU.mult, op1=ALU.mult),
                                     reads=["small", "small_r"], writes=["small_nm"]))
            st_.append(lambda: sc.op("act", lambda e: e.activation(out=xn[:], in_=zsrc, func=ACT.Identity, scale=small[:, 3:4], bias=small[:, 4:5]),
                                     reads=[zkey, "small_r", "small_nm"], writes=["xn"]))
            st_.append(lambda: sc.op("pool", lambda e: e.tensor_tensor(out=xn[:], in0=xn[:], in1=gb[gi][:], op=ALU.mult),
                                     reads=["xn", f"gb{gi}"], writes=["xn"]))

            def last_():
                sc.op("pool", lambda e: e.tensor_tensor(out=xres[:, lb, :], in0=xn[:], in1=gb[bi][:], op=ALU.add),
                      reads=["xn", f"gb{bi}", f"xres{lb}", "xres"], writes=[f"xres{lb}"])
                if final and last:
                    sc.dma("sp", "yout", y[lb * 128:(lb + 1) * 128, :], xres[:, lb, :], reads=[f"xres{lb}"], writes=[f"y{lb}"])
                elif final:
                    sc.dma("sp", "x1out", xown1[lb * 128:(lb + 1) * 128, :], xres[:, lb, :], reads=[f"xres{lb}"], writes=["xown1"])
            st_.append(last_)
            return st_

        def layernorm(zsrc, zkey, gi, bi, lb, final):
            for f_ in ln_steps(zsrc, zkey, gi, bi, lb, final):
                f_()

        sc.op("pool", lambda e: e.memset(small[:, 8:9], LN_EPS), writes=["epsc"])
        dump("d_mixT", mixT[:].rearrange("p c t -> p (c t)"), mixkeys)

        for lb in range(16):
            for half in range(2):
                pb = (lb % 2) * 2 + half
                for mc in range(8):
                    sc.op("pe", lambda e, o=ps[:, pb, :], a=mixT[:, mc, lb * 128:(lb + 1) * 128], b=Wo[:, mc, half * 512:(half + 1) * 512], k=mc:
                          e.matmul(o, lhsT=a, rhs=b, start=(k == 0), stop=(k == 7)),
                          reads=mixkeys + ["Wo"], writes=[psk(pb)])
                sc.op("dve", lambda e, o=zt[:, lb % 2, half * 512:(half + 1) * 512], i0=xres[:, lb, half * 512:(half + 1) * 512], i1=ps[:, pb, :]:
                      e.scalar_tensor_tensor(out=o, in0=i0, scalar=ALPHA, in1=i1, op0=ALU.mult, op1=ALU.add),
                      reads=["xres", f"xres{lb}", psk(pb)], writes=[f"zt{lb % 2}"])
            layernorm(zt[:, lb % 2, :], f"zt{lb % 2}", 0, 1, lb, False)

        dump("d_x1", xres[:].rearrange("p b d -> p (b d)"), [f"xres{lb}" for lb in range(16)])
        sc.barrier()
        sD.close()
        gb[2] = sb2("gb2", [128, D], F32)
        gb[3] = sb2("gb3", [128, D], F32)
        x1T = sb2("x1T", [128, 8, 512], BF16)
        hidT = sb2("hidT", [128, 32, 512], BF16)
        rtmp = [sb2(f"rtmp{i}", [128, 512], F32) for i in range(2)]
        NSLOT = 4
        wsl = [sb2(f"wsl{i}", [128, 8, 512], BF16) for i in range(NSLOT)]
        for i, v in ((2, ln2_g), (3, ln2_b)):
            sc.dma("sp", f"gb{i}", gb[i][:], v[l].partition_broadcast(128), writes=[f"gb{i}"])
        w1v = w1[l].rearrange("(kc p) c -> p kc c", p=128)
        w2v = w2[l].rearrange("(f p) c -> p f c", p=128)
        slot_ctr = [0]

        def wload(src):
            i = slot_ctr[0] % NSLOT
            slot_ctr[0] += 1
            sc.dma("sp", f"wsl{i}", wsl[i][:], src[0], reads=[src[1]], writes=[f"wsl{i}"])
            return i

        wsrcs = []
        for m in range(4):
            for s_ in range(8):
                wsrcs.append((W1b.ap()[s_].rearrange("p (k c) -> p k c", c=512), "W1b"))
            for half in range(2):
                for sc_ in range(4):
                    wsrcs.append((W2b.ap()[half * 4 + sc_].rearrange("p (k c) -> p k c", c=512), "W2b"))
        issued = [0]
        used = [0]

        def wnext():
            while issued[0] < len(wsrcs) and issued[0] < used[0] + NSLOT - 1:
                wload(wsrcs[issued[0]])
                issued[0] += 1
            i = used[0] % NSLOT
            used[0] += 1
            return i

        def x1T_build(m):
            for blk in range(4):
                lb = m * 4 + blk
                pb = 6
                for kc in range(8):
                    sc.op("pe", lambda e, o=ps[:, pb + kc // 4, (kc % 4) * 128:(kc % 4 + 1) * 128],
                          i=xres[:, lb, kc * 128:(kc + 1) * 128]: e.transpose(out=o, in_=i, identity=identf[:]),
                          reads=[f"xres{lb}", "identf"], writes=[psk(pb + kc // 4)])
                src = ps[:, pb:pb + 2, :].rearrange("p a (k t) -> p (a k) t", t=128)
                sc.op("act", lambda e, o=x1T[:, :, blk * 128:(blk + 1) * 128], i=src: e.activation(out=o, in_=i, func=ACT.Copy),
                      reads=[psk(pb), psk(pb + 1)], writes=["x1T"])

        def x2T_gather(g_):
            for blk in range(4):
                lb = g_ * 4 + blk
                pb = 6
                for kc in range(8):
                    sc.op("pe", lambda e, o=ps[:, pb + kc // 4, (kc % 4) * 128:(kc % 4 + 1) * 128],
                          i=xres[:, lb, kc * 128:(kc + 1) * 128]: e.transpose(out=o, in_=i, identity=identf[:]),
                          reads=[f"xres{lb}", "identf"], writes=[psk(pb + kc // 4)])
                src = ps[:, pb:pb + 2, :].rearrange("p a (k t) -> p (a k) t", t=128)
                sc.op("act", lambda e, o=x1T[:, :, blk * 128:(blk + 1) * 128], i=src: e.activation(out=o, in_=i, func=ACT.Copy),
                      reads=[psk(pb), psk(pb + 1)], writes=["x1T"])
            sc.dma("sp", "x2T", aginv[g_], x1T[:], reads=["x1T"], writes=[f"agin{g_}"])
            sc.collective("cc", lambda e, g_=g_: e.collective_compute("AllGather", ALU.bypass, replica_groups=[[0, 1, 2, 3], [4, 5, 6, 7]],
                                                                       ins=[agin[g_].ap().opt()], outs=[agout[g_].ap().opt()], dma_qos="P3"),
                          reads=[f"agin{g_}"], writes=["XT"])

        pend = []
        x1T_build(0)
        for m in range(4):
            for s_ in range(8):
                si = wnext()
                for q in range(4):
                    fc = s_ * 4 + q
                    pb = 4 + fc % 2
                    for kc in range(8):
                        sc.op("pe", lambda e, o=ps[:, pb, :], a=wsl[si][:, kc, q * 128:(q + 1) * 128], b=x1T[:, kc, :], k=kc:
                              e.matmul(o, lhsT=a, rhs=b, start=(k == 0), stop=(k == 7)),
                              reads=[f"wsl{si}", "x1T"], writes=[psk(pb)])
                    rt = rtmp[fc % 2]
                    sc.op("act", lambda e, o=rt[:], i=ps[:, pb, :]: e.activation(out=o, in_=i, func=ACT.Relu),
                          reads=[psk(pb)], writes=[f"rtmp{fc % 2}"])
                    sc.op("dve", lambda e, o=hidT[:, fc, :], i=rt[:]: e.tensor_tensor(out=o, in0=i, in1=i, op=ALU.mult),
                          reads=[f"rtmp{fc % 2}"], writes=[f"hid{fc}"])
                    for _ in range(2):
                        if pend:
                            pend.pop(0)()
            while pend:
                pend.pop(0)()
            if not last and m >= 1:
                x2T_gather(m - 1)
            hidkeys = [f"hid{fc}" for fc in range(32)]
            for half in range(2):
                for sc_ in range(4):
                    si = wnext()
                    for blk in range(4):
                        for fq in range(8):
                            fc = sc_ * 8 + fq
                            sc.op("pe", lambda e, o=ps[:, blk, :], a=hidT[:, fc, blk * 128:(blk + 1) * 128], b=wsl[si][:, fq, :], k=fc:
                                  e.matmul(o, lhsT=a, rhs=b, start=(k == 0), stop=(k == 31)),
                                  reads=hidkeys + [f"wsl{si}"], writes=[psk(blk)])
                for blk in range(4):
                    lb = m * 4 + blk
                    sc.op("dve", lambda e, o=zt[:, blk, half * 512:(half + 1) * 512], i0=xres[:, lb, half * 512:(half + 1) * 512], i1=ps[:, blk, :]:
                          e.scalar_tensor_tensor(out=o, in0=i0, scalar=ALPHA, in1=i1, op0=ALU.mult, op1=ALU.add),
                          reads=[f"xres{lb}", psk(blk)], writes=[f"zt{blk}"])
            if m + 1 < 4:
                x1T_build(m + 1)
            for blk in range(4):
                pend += ln_steps(zt[:, blk, :], f"zt{blk}", 2, 3, m * 4 + blk, True)
        while pend:
            pend.pop(0)()
        if not last:
            x2T_gather(3)
        sc.barrier()


def _core_consts(r):
    k = np.arange(128)[:, None, None]
    jj = np.arange(16)[None, :, None]
    q = np.arange(512)[None, None, :]
    mask = (((jj * 128 + k) <= (r * 512 + q)).astype(np.float32) * 3e38).astype(ml_dtypes.bfloat16).reshape(128, 16 * 512)
    halosel = np.zeros((4, 16), np.float32)
    prefsel = np.zeros((4, 16), np.float32)
    invc = np.zeros((4, 4, 16), np.float32)
    for m in range(4):
        G = 4 * m + r
        if G >= 1:
            halosel[m, G - 1] = 1.0
        prefsel[m, :G] = 1.0
        for g, w in enumerate(WINS):
            pos = G * 512 + np.arange(16) + 1
            invc[m, g] = 1.0 / np.minimum(pos, w)
    return {
        "c_mask": mask,
        "c_halosel": np.ascontiguousarray(np.broadcast_to(halosel.reshape(1, 64), (128, 64))),
        "c_prefsel": np.ascontiguousarray(np.broadcast_to(prefsel.reshape(1, 64), (8, 64))),
        "c_invc": np.ascontiguousarray(np.broadcast_to(invc.reshape(1, 256), (128, 256))),
        "c_identf": np.eye(128, dtype=np.float32),
        "c_swap": np.roll(np.eye(128, dtype=np.float32), 64, axis=0),
    }


def _own_rows(r):
    return np.concatenate([np.arange((4 * m + r) * 512, (4 * m + r + 1) * 512) for m in range(4)])


_PROG = {}
FUSED = True


def kernel(x, w_in, b_f, pool_w, pool_scale, w_out, ln1_g, ln1_b, w_mlp1, w_mlp2, ln2_g, ln2_b):
    f32 = lambda a: np.ascontiguousarray(np.asarray(a, dtype=np.float32))
    x = f32(x)
    ws = dict(w_in=f32(w_in), b_f=f32(b_f), pool_w=f32(pool_w), pool_scale=f32(pool_scale), w_out=f32(w_out),
              ln1_g=f32(ln1_g), ln1_b=f32(ln1_b), w_mlp1=f32(w_mlp1), w_mlp2=f32(w_mlp2), ln2_g=f32(ln2_g), ln2_b=f32(ln2_b))
    consts = [_core_consts(c % 4) for c in range(NCORES)]
    rows = [_own_rows(c % 4) for c in range(NCORES)]
    if FUSED:
        if "fused" not in _PROG:
            _PROG["fused"] = build_program(DEPTH, True)
        nc = _PROG["fused"]
        in_maps = []
        for c in range(NCORES):
            b = c // 4
            m = {"xfull": x[b], "xown": np.ascontiguousarray(x[b][rows[c]])}
            m.update(ws)
            m.update(consts[c])
            in_maps.append(m)
        res = run_bass_kernel_spmd(nc, in_maps, core_ids=list(range(NCORES)))
        out = np.empty_like(x)
        for c in range(NCORES):
            out[c // 4][rows[c]] = res.results[c]["y"]
        return out
    if "unfused" not in _PROG:
        _PROG["unfused"] = build_program(1, False)
    nc = _PROG["unfused"]
    cur = x
    for l in range(DEPTH):
        in_maps = []
        for c in range(NCORES):
            b = c // 4
            m = {"xfull": cur[b], "xown": np.ascontiguousarray(cur[b][rows[c]])}
            for k, v in ws.items():
                m[k] = np.ascontiguousarray(v[l:l + 1])
            m.update(consts[c])
            in_maps.append(m)
        res = run_bass_kernel_spmd(nc, in_maps, core_ids=list(range(NCORES)))
        nxt = np.empty_like(cur)
        for c in range(NCORES):
            nxt[c // 4][rows[c]] = res.results[c]["y"]
        cur = nxt
    return cur
```
